# Optimizing a Trainium2 kernel written in Bass

```python
import jax, jax.numpy as jnp
from jax import lax
import numpy as np

D_MODEL = 2048
BATCH = 4
SEQ = 2048
DEPTH = 1
DEC_BATCH = 8
DEC_SEQ = 4
PAST_LEN = 16384
PAGE_SIZE = 128

N_HEADS = 8
HEAD_DIM = D_MODEL // 16
ATTN_WIDTH = N_HEADS * HEAD_DIM
MOBA_BLOCK = 256
MOBA_TOPK = 3
Q_BLOCK = 16
ROPE_THETA = 500000.0
ROPE_DIM = HEAD_DIM // 4
GMLP_GROUPS = 8
GMLP_CHUNK = 128
GMLP_WIDTH = D_MODEL // 2
GMLP_GROUP_DIM = GMLP_WIDTH // GMLP_GROUPS
IN_WIDTH = 3 * ATTN_WIDTH + 2 * GMLP_WIDTH
FFN_HIDDEN = -(-8 * D_MODEL // (3 * 256)) * 256
PLE_DIM = 256
NORM_EPS = 1e-6
NEG_INF = -1e30

kernel_name = "moba_gmlp_gated_hybrid_step"


def rms_norm(x, g):
    xf = x.astype(jnp.float32)
    y = xf * lax.rsqrt(jnp.mean(xf * xf, axis=-1, keepdims=True) + NORM_EPS)
    return (y * g.astype(jnp.float32)).astype(x.dtype)


def layer_norm(x, g):
    xf = x.astype(jnp.float32)
    xc = xf - jnp.mean(xf, axis=-1, keepdims=True)
    y = xc * lax.rsqrt(jnp.mean(xc * xc, axis=-1, keepdims=True) + NORM_EPS)
    return (y * g.astype(jnp.float32)).astype(x.dtype)


def partial_rope(x, pos):
    half = ROPE_DIM // 2
    freqs = jnp.power(jnp.float32(ROPE_THETA), -2.0 * jnp.arange(half, dtype=jnp.float32) / ROPE_DIM)
    ang = pos.astype(jnp.float32)[:, None] * freqs[None, :]
    cos = jnp.cos(ang)[None, :, None, :]
    sin = jnp.sin(ang)[None, :, None, :]
    xf = x.astype(jnp.float32)
    x1 = xf[..., :half]
    x2 = xf[..., half:ROPE_DIM]
    out = jnp.concatenate([x1 * cos - x2 * sin, x2 * cos + x1 * sin, xf[..., ROPE_DIM:]], axis=-1)
    return out.astype(x.dtype)


def to_blocks(parts):
    B, _, H, hd = parts[0].shape
    L = sum(p.shape[1] for p in parts)
    nb = -(-L // MOBA_BLOCK)
    pad = nb * MOBA_BLOCK - L
    full = jnp.concatenate(list(parts) + [jnp.zeros((B, pad, H, hd), parts[0].dtype)], axis=1)
    return full.reshape(B, nb, MOBA_BLOCK, H, hd)


def moba_block(q, pos, kb, vb, kmean):
    B, Qb, H, hd = q.shape
    NB = kb.shape[1]
    qh = q.transpose(0, 2, 1, 3)
    gate = jnp.einsum('bhqd,bnhd->bhqn', qh.astype(jnp.float32), kmean)
    qblk = pos // MOBA_BLOCK
    past_ok = jnp.arange(NB)[None, :] < qblk[:, None]
    gate = jnp.where(past_ok[None, None], gate, NEG_INF)
    ksel = min(MOBA_TOPK, NB)
    _, sel = lax.top_k(gate, ksel)
    sel_ok = sel < qblk[None, None, :, None]
    own = jnp.broadcast_to(qblk[None, None, :, None], (B, H, Qb, 1)).astype(sel.dtype)
    idx = jnp.concatenate([sel, own], axis=-1)
    slot_ok = jnp.concatenate([sel_ok, jnp.ones((B, H, Qb, 1), bool)], axis=-1)
    bi = jnp.arange(B)[:, None, None, None]
    hi = jnp.arange(H)[None, :, None, None]
    kg = kb[bi, idx, :, hi]
    vg = vb[bi, idx, :, hi]
    kpos = idx[..., None] * MOBA_BLOCK + jnp.arange(MOBA_BLOCK)
    mask = slot_ok[..., None] & (kpos <= pos[None, None, :, None, None])
    s = jnp.einsum('bhqd,bhqskd->bhqsk', qh, kg, preferred_element_type=jnp.float32) * (HEAD_DIM ** -0.5)
    s = jnp.where(mask, s, NEG_INF)
    nslot = idx.shape[-1]
    p = jax.nn.softmax(s.reshape(B, H, Qb, nslot * MOBA_BLOCK), axis=-1).reshape(s.shape)
    o = jnp.einsum('bhqsk,bhqskd->bhqd', p.astype(vg.dtype), vg)
    return o.transpose(0, 2, 1, 3)


def moba_prompt(q, k, v, pos):
    B, S, H, hd = q.shape
    kb = to_blocks([k])
    vb = to_blocks([v])
    kmean = jnp.mean(kb.astype(jnp.float32), axis=2)
    nqb = S // Q_BLOCK
    qs = q.reshape(B, nqb, Q_BLOCK, H, hd).transpose(1, 0, 2, 3, 4)
    ps = pos.reshape(nqb, Q_BLOCK)
    out = lax.map(lambda a: moba_block(a[0], a[1], kb, vb, kmean), (qs, ps))
    return out.transpose(1, 0, 2, 3, 4).reshape(B, S, H, hd)


def moba_sample(q, past_k, k, past_v, v, pos):
    kb = to_blocks([past_k, k])
    vb = to_blocks([past_v, v])
    kmean = jnp.mean(kb.astype(jnp.float32), axis=2)
    return moba_block(q, pos, kb, vb, kmean)


def spatial_gating(u, vn, w_s, b_s):
    B, S, W = vn.shape
    n = min(S, GMLP_CHUNK)
    nc = S // n
    wm = jnp.tril(w_s[:, :n, :n])
    vr = vn.reshape(B, nc, n, GMLP_GROUPS, GMLP_GROUP_DIM)
    s = jnp.einsum('gts,bcsgd->bctgd', wm, vr) + b_s[:, :n].T[None, None, :, :, None]
    return u * s.reshape(B, S, W)


def mixer_inputs(x, pos, g_pre, w_in, g_vn):
    B, S, _ = x.shape
    h = rms_norm(x, g_pre)
    z = h @ w_in
    q, k, v, u, vg = jnp.split(z, [ATTN_WIDTH, 2 * ATTN_WIDTH, 3 * ATTN_WIDTH, 3 * ATTN_WIDTH + GMLP_WIDTH], axis=-1)
    q = partial_rope(q.reshape(B, S, N_HEADS, HEAD_DIM), pos)
    k = partial_rope(k.reshape(B, S, N_HEADS, HEAD_DIM), pos)
    v = v.reshape(B, S, N_HEADS, HEAD_DIM)
    u = jax.nn.gelu(u)
    vn = layer_norm(jax.nn.gelu(vg), g_vn)
    return h, q, k, v, u, vn


def layer_tail(x, h, attn_o, gm_o, p, w_a_out, w_b_out, w_gate, w_o, g_post_mix,
               g_pre_ffn, w_ffn_in, w_ffn_out, g_post_ffn, g_ple, w_ple_gate, w_ple):
    B, S, _ = x.shape
    a = attn_o.reshape(B, S, ATTN_WIDTH) @ w_a_out
    b = gm_o @ w_b_out
    ga, gb = jnp.split(jax.nn.sigmoid(h @ w_gate), 2, axis=-1)
    mix = (ga * a + gb * b) @ w_o
    x = x + rms_norm(mix, g_post_mix)
    a1, g1 = jnp.split(rms_norm(x, g_pre_ffn) @ w_ffn_in, 2, axis=-1)
    f = (jax.nn.silu(a1) * g1) @ w_ffn_out
    x = x + rms_norm(f, g_post_ffn)
    gate = jax.nn.sigmoid(rms_norm(x, g_ple) @ w_ple_gate)
    return x + (p @ w_ple) * gate


def setup_inputs(seed: int = 0) -> dict:
    key = jax.random.key(seed)
    ks = jax.random.split(key, 32)
    f32 = jnp.float32
    n_pages = PAST_LEN // PAGE_SIZE
    n_used = DEC_BATCH * n_pages
    n_pool = n_used + max(1, n_used // 4)

    def nrm(k, shape, scale):
        return jax.random.normal(k, shape, f32) * scale

    def gain(k, shape):
        return 1.0 + 0.05 * jax.random.normal(k, shape, f32)

    page_table = jax.random.permutation(ks[0], n_pool)[:n_used].reshape(DEC_BATCH, n_pages).astype(jnp.int32)
    return {
        "x_prompt": nrm(ks[1], (BATCH, SEQ, D_MODEL), 1.0),
        "x_sample": nrm(ks[2], (DEC_BATCH, DEC_SEQ, D_MODEL), 1.0),
        "cache_k": nrm(ks[3], (DEPTH, n_pool, PAGE_SIZE, N_HEADS, HEAD_DIM), 1.0),
        "cache_v": nrm(ks[4], (DEPTH, n_pool, PAGE_SIZE, N_HEADS, HEAD_DIM), 1.0),
        "page_table": page_table,
        "p_prompt": nrm(ks[5], (DEPTH, BATCH, SEQ, PLE_DIM), 1.0),
        "p_sample": nrm(ks[6], (DEPTH, DEC_BATCH, DEC_SEQ, PLE_DIM), 1.0),
        "g_pre_mix": gain(ks[7], (DEPTH, D_MODEL)),
        "w_in": nrm(ks[8], (DEPTH, D_MODEL, IN_WIDTH), D_MODEL ** -0.5),
        "g_vnorm": gain(ks[9], (DEPTH, GMLP_WIDTH)),
        "w_spatial": nrm(ks[10], (DEPTH, GMLP_GROUPS, GMLP_CHUNK, GMLP_CHUNK), GMLP_CHUNK ** -0.5),
        "b_spatial": 1.0 + 0.1 * jax.random.normal(ks[11], (DEPTH, GMLP_GROUPS, GMLP_CHUNK), f32),
        "w_a_out": nrm(ks[12], (DEPTH, ATTN_WIDTH, D_MODEL), ATTN_WIDTH ** -0.5),
        "w_b_out": nrm(ks[13], (DEPTH, GMLP_WIDTH, D_MODEL), GMLP_WIDTH ** -0.5),
        "w_gate": nrm(ks[14], (DEPTH, D_MODEL, 2 * D_MODEL), D_MODEL ** -0.5),
        "w_o": nrm(ks[15], (DEPTH, D_MODEL, D_MODEL), D_MODEL ** -0.5),
        "g_post_mix": gain(ks[16], (DEPTH, D_MODEL)),
        "g_pre_ffn": gain(ks[17], (DEPTH, D_MODEL)),
        "w_ffn_in": nrm(ks[18], (DEPTH, D_MODEL, 2 * FFN_HIDDEN), D_MODEL ** -0.5),
        "w_ffn_out": nrm(ks[19], (DEPTH, FFN_HIDDEN, D_MODEL), FFN_HIDDEN ** -0.5),
        "g_post_ffn": gain(ks[20], (DEPTH, D_MODEL)),
        "g_ple": gain(ks[21], (DEPTH, D_MODEL)),
        "w_ple_gate": nrm(ks[22], (DEPTH, D_MODEL, D_MODEL), D_MODEL ** -0.5),
        "w_ple": nrm(ks[23], (DEPTH, PLE_DIM, D_MODEL), PLE_DIM ** -0.5),
    }


def reference(x_prompt, x_sample, cache_k, cache_v, page_table, p_prompt, p_sample,
              g_pre_mix, w_in, g_vnorm, w_spatial, b_spatial, w_a_out, w_b_out, w_gate, w_o,
              g_post_mix, g_pre_ffn, w_ffn_in, w_ffn_out, g_post_ffn, g_ple, w_ple_gate, w_ple):
    n_seq = x_prompt.shape[1]
    dec_b, dec_s = x_sample.shape[0], x_sample.shape[1]
    past_len = page_table.shape[1] * cache_k.shape[2]
    pos_p = jnp.arange(n_seq, dtype=jnp.int32)
    pos_s = past_len + jnp.arange(dec_s, dtype=jnp.int32)
    xp, xs = x_prompt, x_sample
    kp_rows, vp_rows, ks_rows, vs_rows, gv_rows = [], [], [], [], []
    for l in range(DEPTH):
        hp, qp, kp, vp, up, vnp = mixer_inputs(xp, pos_p, g_pre_mix[l], w_in[l], g_vnorm[l])
        ap = moba_prompt(qp, kp, vp, pos_p)
        gp = spatial_gating(up, vnp, w_spatial[l], b_spatial[l])
        xp = layer_tail(xp, hp, ap, gp, p_prompt[l], w_a_out[l], w_b_out[l], w_gate[l], w_o[l],
                        g_post_mix[l], g_pre_ffn[l], w_ffn_in[l], w_ffn_out[l], g_post_ffn[l],
                        g_ple[l], w_ple_gate[l], w_ple[l])
        hs, qs, ks_, vs_, us, vns = mixer_inputs(xs, pos_s, g_pre_mix[l], w_in[l], g_vnorm[l])
        past_k = cache_k[l][page_table].reshape(dec_b, past_len, N_HEADS, HEAD_DIM)
        past_v = cache_v[l][page_table].reshape(dec_b, past_len, N_HEADS, HEAD_DIM)
        as_ = moba_sample(qs, past_k, ks_, past_v, vs_, pos_s)
        gs = spatial_gating(us, vns, w_spatial[l], b_spatial[l])
        xs = layer_tail(xs, hs, as_, gs, p_sample[l], w_a_out[l], w_b_out[l], w_gate[l], w_o[l],
                        g_post_mix[l], g_pre_ffn[l], w_ffn_in[l], w_ffn_out[l], g_post_ffn[l],
                        g_ple[l], w_ple_gate[l], w_ple[l])
        kp_rows.append(kp)
        vp_rows.append(vp)
        ks_rows.append(ks_)
        vs_rows.append(vs_)
        gv_rows.append(vns)
    return (xp, xs, jnp.stack(kp_rows), jnp.stack(vp_rows), jnp.stack(ks_rows), jnp.stack(vs_rows), jnp.stack(gv_rows))
```

```python
import numpy as np
import ml_dtypes
import concourse.bass as bass
import concourse.mybir as mybir
from concourse.bass_utils import run_bass_kernel_spmd

F32 = mybir.dt.float32
F32R = mybir.dt.float32r
BF16 = mybir.dt.bfloat16
I32 = mybir.dt.int32
U32 = mybir.dt.uint32
AF = mybir.ActivationFunctionType
ALU = mybir.AluOpType
AX = mybir.AxisListType

D = 2048
NH = 8
HD = 128
NT = 1028
NTP = 1024
FFN = 5632
NKC = 16
WB = 256
EPS = 1e-6
NEG = -1e30


class Ins:
    __slots__ = ("eng", "fn", "idx", "dma", "deps", "signal", "sem", "val", "key")

    def __init__(self, eng, fn, idx, dma, key):
        self.eng, self.fn, self.idx, self.dma, self.key = eng, fn, idx, dma, key
        self.deps = set()
        self.signal = False
        self.sem = None
        self.val = 0


class Prog:
    ENGS = ("pe", "act", "dve", "pool", "sp")

    def __init__(self, nc):
        self.nc = nc
        self.streams = {e: [] for e in self.ENGS}
        self.lw = {}
        self.rd = {}
        self.dma_keys = {}
        self.bufreg = {}
        self.active = set()
        self.bufres = {}
        self.pending = {}

    def region(self, buf, lo, hi):
        self.bufreg[buf] = (lo, hi)

    def _bufof(self, x):
        b = x[0] if isinstance(x, tuple) else x
        return b if b in self.bufreg else None

    def _compress(self, users):
        best = {}
        out = []
        seen = set()
        for u in users:
            if u.dma:
                if id(u) not in seen:
                    seen.add(id(u))
                    out.append(u)
            else:
                b = best.get(u.eng)
                if b is None or b.idx < u.idx:
                    best[u.eng] = u
        return out + list(best.values())

    def _touch(self, B):
        if B in self.active:
            return
        lo, hi = self.bufreg[B]
        users = list(self.pending.get(B, ()))
        for O in list(self.active):
            olo, ohi = self.bufreg[O]
            if olo < hi and lo < ohi:
                for res in self.bufres.get(O, ()):
                    l = self.lw.pop(res, None)
                    if l is not None:
                        users.append(l)
                    users.extend(self.rd.pop(res, ()))
                users.extend(self.pending.pop(O, ()))
                self.bufres[O] = set()
                self.active.discard(O)
        self.pending[B] = self._compress(users)
        self.active.add(B)

    def op(self, eng, fn, r=(), w=(), dma=None):
        ins = Ins(eng, fn, len(self.streams[eng]), dma is not None, dma)
        deps = ins.deps
        for x in tuple(r) + tuple(w):
            B = self._bufof(x)
            if B is not None:
                self._touch(B)
                self.bufres.setdefault(B, set()).add(x)
                pend = self.pending.get(B)
                if pend:
                    deps.update(pend)
        for x in r:
            l = self.lw.get(x)
            if l is not None:
                deps.add(l)
        for x in w:
            l = self.lw.get(x)
            if l is not None:
                deps.add(l)
            for q in self.rd.get(x, ()):
                deps.add(q)
        deps.discard(ins)
        for x in r:
            self.rd.setdefault(x, []).append(ins)
        for x in w:
            self.lw[x] = ins
            self.rd[x] = []
        self.streams[eng].append(ins)
        return ins

    def emit(self, final_waits):
        nc = self.nc
        for e in self.ENGS:
            for ins in self.streams[e]:
                for d in ins.deps:
                    if d.eng == "pe" and ins.eng == "pe" and not d.dma:
                        continue
                    d.signal = True
        for ins in final_waits:
            ins.signal = True
        import contextlib
        with contextlib.ExitStack() as st:
            esem = {e: st.enter_context(nc.semaphore("sem_" + e)) for e in self.ENGS}
            ksem = {}
            kcnt = {}
            for e in self.ENGS:
                cnt = 0
                for ins in self.streams[e]:
                    if ins.dma:
                        if ins.key not in ksem:
                            ksem[ins.key] = st.enter_context(nc.semaphore("dq_%d" % len(ksem)))
                            kcnt[ins.key] = 0
                        kcnt[ins.key] += 16
                        ins.sem, ins.val = ksem[ins.key], kcnt[ins.key]
                    elif ins.signal:
                        cnt += 1
                        ins.sem, ins.val = esem[e], cnt
            handles = {"pe": "tensor", "act": "scalar", "dve": "vector", "pool": "gpsimd", "sp": "sync"}
            block = st.enter_context(nc.Block())

            def make(e):
                def body(eng):
                    waited = {}
                    for ins in self.streams[e]:
                        need = {}
                        for d in ins.deps:
                            if d.eng == "pe" and e == "pe" and not d.dma:
                                continue
                            k = id(d.sem)
                            if need.get(k, (None, 0))[1] < d.val:
                                need[k] = (d.sem, d.val)
                        for k, (s, v) in need.items():
                            if waited.get(k, 0) >= v:
                                continue
                            eng.wait_ge(s, v)
                            waited[k] = v
                        bi = ins.fn(eng)
                        if ins.dma:
                            bi.then_inc(ins.sem, 16)
                        elif ins.signal:
                            bi.then_inc(ins.sem, 1)
                    if e == "sp":
                        need = {}
                        for d in final_waits:
                            k = id(d.sem)
                            if need.get(k, (None, 0))[1] < d.val:
                                need[k] = (d.sem, d.val)
                        for k, (s, v) in need.items():
                            eng.wait_ge(s, v)
                return body

            for e in self.ENGS:
                getattr(block, handles[e])(make(e))
            st.close()


def rope_tables(pos):
    half = 16
    freqs = np.power(np.float32(500000.0), (-2.0 * np.arange(half, dtype=np.float32) / 32).astype(np.float32)).astype(np.float32)
    ang = pos.astype(np.float32)[None, :] * freqs[:, None]
    c = np.cos(ang).astype(np.float32)
    s = np.sin(ang).astype(np.float32)
    return np.concatenate([c, c], 0), np.concatenate([s, s], 0)


def emit_sample_scan_chunk(P, nc, c, H):
    gb = H["gbuf"][c % 2]
    res = ("gbuf", c % 2)
    acc2 = H["acc2"]
    P.op("pool", lambda e: e.indirect_dma_start(
        out=gb, out_offset=None, in_=H["cache_k_pages"],
        in_offset=bass.IndirectOffsetOnAxis(ap=H["pt128"][:, 0:1], axis=0), element_offset=c * 4096),
        r=["pt128"], w=[res], dma=res)
    if c == 0:
        P.op("dve", lambda e: e.tensor_tensor(out=acc2[:, :], in0=gb[:, 0:2048], in1=gb[:, 2048:4096], op=ALU.add), [res], ["acc2"])
    else:
        P.op("pool", lambda e: e.tensor_tensor(out=gb[:, 0:2048], in0=gb[:, 0:2048], in1=gb[:, 2048:4096], op=ALU.add), [res], [res])
        P.op("dve", lambda e: e.tensor_tensor(out=acc2[:, :], in0=acc2[:, :], in1=gb[:, 0:2048], op=ALU.add), [res, "acc2"], ["acc2"])


def emit_sample_attention(P, nc, H, bank):
    psA = H["psA"]
    SCALE = float(HD) ** -0.5

    def op(eng, fn, r, w):
        return P.op(eng, fn, r, w)

    def COPY(eng, out, in_, r, w):
        if eng == "act":
            return P.op("act", lambda e: e.activation(out=out, in_=in_, func=AF.Copy), r, w)
        return P.op(eng, lambda e: e.tensor_copy(out=out, in_=in_), r, w)

    def TT(eng, out, in0, in1, o, r, w):
        return P.op(eng, lambda e: e.tensor_tensor(out=out, in0=in0, in1=in1, op=o), r, w)

    def MM(out, lhsT, rhs, start, stop, r, w):
        return P.op("pe", lambda e: e.matmul(out, lhsT=lhsT, rhs=rhs, start=start, stop=stop), r, w)

    def dma(eng, out, in_, r, w, key):
        return P.op(eng, lambda e: e.dma_start(out=out, in_=in_), r=r, w=w, dma=key)

    kms_sb, kmsf, kmst, kmshi, kmslo = H["kms_sb"], H["kmsf"], H["kmst"], H["kmshi"], H["kmslo"]
    ident_f = H["ident_f"]
    acc2 = H["acc2"]
    TT("dve", acc2[:, 0:1024], acc2[:, 0:1024], acc2[:, 1024:2048], ALU.add, ["acc2"], ["acc2"])
    for half in range(2):
        b = bank()
        MM(psA[0:64, b, 0:512], H["pair"][:, :], acc2[:, half * 512:(half + 1) * 512], True, True, ["acc2", "pair"], [("ps", b)])
        COPY("act", kms_sb[0:64, half * 512:(half + 1) * 512], psA[0:64, b, :], [("ps", b)], ["kms_sb", ("ps", b)])
    for h in range(8):
        b = bank()
        P.op("pe", lambda e, b=b, h=h: e.transpose(psA[:, b, 0:64], kms_sb[0:64, h * 128:(h + 1) * 128], ident_f[0:64, 0:64]),
             ["kms_sb", "ident_f"], [("ps", b)])
        COPY("dve", kmsf[:, h, :], psA[:, b, 0:64], [("ps", b)], ["kmsf", ("ps", b)])
    COPY("dve", kmshi[:], kmsf[:], ["kmsf"], ["kmshi"])
    COPY("dve", kmst[:], kmshi[:], ["kmshi"], ["kmst"])
    TT("dve", kmst[:], kmsf[:], kmst[:], ALU.subtract, ["kmsf", "kmst"], ["kmst"])
    COPY("dve", kmslo[:], kmst[:], ["kmst"], ["kmslo"])
    if H.get('stop', 99) <= 1:
        return
    qsT = H["qsT"]
    b = bank()
    for h in range(8):
        MM(psA[0:4, b, h * 64:(h + 1) * 64], qsT[:, h, 0:4], kmshi[:, h, :], True, False, ["qsT", "kmshi"], [("ps", b)])
        MM(psA[0:4, b, h * 64:(h + 1) * 64], qsT[:, h, 0:4], kmslo[:, h, :], False, True, ["qsT", "kmslo"], [("ps", b)])
    gs, top8s, idx8, idx3c = H["gs"], H["top8s"], H["idx8"], H["idx3c"]
    COPY("act", gs[0:4, :], psA[0:4, b, 0:512], [("ps", b)], ["gs", ("ps", b)])
    for h in range(8):
        P.op("dve", lambda e, h=h: e.max(out=top8s[0:4, h * 8:(h + 1) * 8], in_=gs[0:4, h * 64:(h + 1) * 64]), ["gs"], ["top8s"])
        P.op("dve", lambda e, h=h: e.max_index(out=idx8[0:4, h * 8:(h + 1) * 8], in_max=top8s[0:4, h * 8:(h + 1) * 8],
                                             in_values=gs[0:4, h * 64:(h + 1) * 64]), ["gs", "top8s"], ["idx8"])
    COPY("dve", idx3c[0:4, :].rearrange("p (h s) -> p h s", s=3), idx8[0:4, :].rearrange("p (h k) -> p h k", k=8)[:, :, 0:3],
         ["idx8"], ["idx3c"])
    if H.get('stop', 99) <= 2:
        return
    nb, pg, rb96, rbb, ridx = H["nb"], H["pg"], H["rb96"], H["rbb"], H["ridx"]
    P.op("dve", lambda e: e.memset(nb[:, :].bitcast(F32), 0.0), [], ["nb"])
    dma("sp", H["s_idx"], idx3c[0:4, :], ["idx3c"], ["s_idx"], "s_idx")
    dma("sp", nb[0:96, :], H["s_idx"].rearrange("q (c o) -> (q c) o", o=1), ["s_idx"], ["nb"], "nb")
    P.op("pool", lambda e: e.indirect_dma_start(
        out=pg[:, :], out_offset=None, in_=H["ptp"], in_offset=bass.IndirectOffsetOnAxis(ap=nb[:, 0:1], axis=0)),
        r=["nb"], w=["pg"], dma="pg")
    pgf = H["pgf"]
    COPY("dve", pgf[:, :], pg[:, :], ["pg"], ["pgf"])
    P.op("dve", lambda e: e.tensor_scalar(out=rb96[:, :], in0=pgf[:, :], scalar1=128.0, scalar2=None, op0=ALU.mult), ["pgf"], ["rb96"])
    dma("sp", H["s_pg"], rb96[0:96, :], ["rb96"], ["s_pg"], "s_pg")
    dma("sp", rbb[:, :], H["s_pg"].rearrange("a (b o) -> o (a b)", o=1).partition_broadcast(128), ["s_pg"], ["rbb"], "rbb")
    TT("dve", rbb[:, :], rbb[:, :], H["iota"][:, :], ALU.add, ["rbb", "iota"], ["rbb"])
    COPY("dve", ridx[:, :], rbb[:, :], ["rbb"], ["ridx"])
    if H.get('stop', 99) <= 3:
        return
    kT, Vs = H["kT"], H["Vs"]
    bo = bank()
    for q in range(4):
        for h in range(8):
            col = q * 8 + h
            MM(psA[0:4, bo, col:col + 1], kT[:, h, 2048:2052], qsT[:, h, q:q + 1], True, True, ["qsT", "kTs"], [("ps", bo)])
    Pown, Pownb = H["Pown"], H["Pownb"]
    P.op("act", lambda e: e.activation(out=Pown[0:4, :], in_=psA[0:4, bo, 0:32], func=AF.Exp, scale=SCALE), [("ps", bo)],
         ["Pown", ("ps", bo)])
    TT("dve", Pown[0:4, :], Pown[0:4, :], H["cm"][0:4, :], ALU.mult, ["Pown", "cm"], ["Pown"])
    COPY("dve", Pownb[0:4, :], Pown[0:4, :], ["Pown"], ["Pownb"])
    if H.get('stop', 99) <= 4:
        return
    Kg, Vg, tmpk, S, Pm, Pr = H["Kg"], H["Vg"], H["tmpk"], H["S"], H["Pm"], H["Pr"]
    qbc = H["qbc"]
    NKB, NVB = Kg.shape[1], Vg.shape[1]
    bO = bank()
    for q in range(4):
        for h in range(8):
            col = q * 8 + h
            js = [col * 6 + i for i in range(6)]
            for j in js:
                kb = j % NKB
                P.op("pool", lambda e, j=j, kb=kb, h=h: e.indirect_dma_start(
                    out=Kg[:, kb, :], out_offset=None, in_=H["cache_k_rows"],
                    in_offset=bass.IndirectOffsetOnAxis(ap=ridx[:, j:j + 1], axis=0), element_offset=h * 128),
                    r=["ridx"], w=[("Kg", kb)], dma=("Kg", kb))
                vb = j % NVB
                P.op("pool", lambda e, j=j, vb=vb, h=h: e.indirect_dma_start(
                    out=Vg[:, vb, :], out_offset=None, in_=H["cache_v_rows"],
                    in_offset=bass.IndirectOffsetOnAxis(ap=ridx[:, j:j + 1], axis=0), element_offset=h * 128),
                    r=["ridx"], w=[("Vg", vb)], dma=("Vg", vb))
                TT("dve", tmpk[:, :], Kg[:, kb, :], qbc[:, col * 128:(col + 1) * 128], ALU.mult, [("Kg", kb), "qbc"], ["tmpk"])
                P.op("dve", lambda e, j=j: e.tensor_reduce(out=S[:, j:j + 1], in_=tmpk[:, :], axis=AX.X, op=ALU.add),
                     ["tmpk"], [("S", col)])
            P.op("act", lambda e, col=col: e.activation(out=Pm[:, col * 6:(col + 1) * 6], in_=S[:, col * 6:(col + 1) * 6],
                                                       func=AF.Exp, scale=SCALE), [("S", col)], [("Pm", col)])
            for i, j in enumerate(js):
                vb = j % NVB
                MM(psA[:, bO, col:col + 1], Vg[:, vb, :], Pm[:, j:j + 1], i == 0, False, [("Vg", vb), ("Pm", col)], [("ps", bO)])
            MM(psA[:, bO, col:col + 1], Vs[0:4, h, 0:128], Pownb[0:4, col:col + 1], False, True, ["Vs", "Pownb"], [("ps", bO)])
    if H.get('stop', 99) <= 5:
        return
    Pmres = [("Pm", c) for c in range(32)]
    P.op("dve", lambda e: e.tensor_reduce(out=Pr[:, :], in_=Pm[:, :].rearrange("p (c s) -> p c s", s=6), axis=AX.X, op=ALU.add),
         Pmres, ["Pr"])
    bD = bank()
    MM(psA[:, bD, 0:32], H["ones_f"][:, :], Pr[:, :], True, False, ["Pr", "ones_f"], [("ps", bD)])
    MM(psA[:, bD, 0:32], H["ones_f"][0:4, :], Pown[0:4, :], False, True, ["Pown", "ones_f"], [("ps", bD)])
    rden = H["rden"]
    P.op("dve", lambda e: e.reciprocal(out=rden[:, :], in_=psA[:, bD, 0:32]), [("ps", bD)], ["rden", ("ps", bD)])
    TT("dve", H["OTs"][:, :], psA[:, bO, 0:32], rden[:, :], ALU.mult, [("ps", bO), "rden"], ["OTs", ("ps", bO)])
def build(with_sample=True):
    import contextlib
    nc = bass.Bass("TRN2", target_bir_lowering=False)
    P = Prog(nc)
    st = contextlib.ExitStack()

    def din(name, shape, dt=F32):
        return nc.dram_tensor(name, list(shape), dt, kind="ExternalInput").ap()

    def dout(name, shape, dt=F32):
        return nc.dram_tensor(name, list(shape), dt, kind="ExternalOutput").ap()

    def sb(name, shape, dt):
        return st.enter_context(nc.sbuf_tensor(name, list(shape), dt))

    x_own = din("x_own", [NT, D])
    x_prev = din("x_prev", [NTP, D])
    p_own = din("p_own", [NT, 256])
    w_in = din("w_in", [D, 5120])
    w_gate = din("w_gate", [D, 4096])
    w_a_out = din("w_a_out", [1024, D])
    w_b_out = din("w_b_out", [1024, D])
    w_o = din("w_o", [D, D])
    w_ffn_in = din("w_ffn_in", [D, 2 * FFN])
    w_ffn_out = din("w_ffn_out", [FFN, D])
    w_ple_gate = din("w_ple_gate", [D, D])
    w_ple = din("w_ple", [256, D])
    w_spatial = din("w_spatial", [8, 128, 128])
    b_spatial = din("b_spatial", [1, 1024])
    gvecs = {n: din(n, [1, D]) for n in ("g_pre_mix", "g_post_mix", "g_pre_ffn", "g_post_ffn", "g_ple")}
    g_vnorm = din("g_vnorm", [1, 1024])
    ropeC_own = din("ropeC_own", [32, NT])
    ropeS_own = din("ropeS_own", [32, NT])
    ropeC_prev = din("ropeC_prev", [32, NTP])
    ropeS_prev = din("ropeS_prev", [32, NTP])
    c_ident_bf = din("c_ident_bf", [128, 128], BF16)
    c_ident_f = din("c_ident_f", [128, 128])
    c_perm = din("c_perm", [32, 32], BF16)
    c_tri_bf = din("c_tri_bf", [128, 128], BF16)
    c_tri_f = din("c_tri_f", [128, 128])
    c_negm = din("c_negm", [1, 4 * 64])
    cache_k = din("cache_k", [1280 * 128, 1024])
    cache_v = din("cache_v", [1280 * 128, 1024])
    ptp_d = din("ptp", [64, 2], I32)
    pt128_d = din("pt128", [128, 1], U32)
    c_pair = din("c_pair", [128, 64])
    c_iota = din("c_iota", [128, 192])
    c_cm = din("c_cm", [4, 32])
    c_ones = din("c_ones", [128, 128])
    s_idx = nc.dram_tensor("s_idx", [4, 24], U32).ap()
    s_pg = nc.dram_tensor("s_pg", [96, 2], F32).ap()
    s_q = nc.dram_tensor("s_q", [4, 1024], F32).ap()

    o_k = dout("o_k", [NT, 1024])
    o_v = dout("o_v", [NT, 1024])
    o_gv = dout("o_gv", [4, 1024])
    o_y = dout("o_y", [NT, D])

    outs = []

    ident_bf = sb("ident_bf", [128, 128], BF16)
    ident_f = sb("ident_f", [128, 128], F32)
    perm = sb("perm", [32, 32], BF16)
    tri_bf = sb("tri_bf", [128, 128], BF16)
    tri_f = sb("tri_f", [128, 128], F32)
    negm = sb("negm", [128, 4 * 64], F32)
    NS = 3
    wbuf = sb("wbuf", [128, NS, NKC, WB], BF16)
    gbc = sb("gbc", [128, D], F32)
    kT = sb("kT", [128, NH, 2048 + 4], BF16)
    Vaug = sb("Vaug", [128, 16, NH, 130], BF16)
    Vs = sb("Vs", [4, NH, 130], BF16)
    stat = sb("stat", [128, 256], F32)
    hbt = sb("hbt", [128, D], BF16)
    junk = hbt
    WT = sb("WT", [128, 8, 128], BF16)
    bsb = sb("bsb", [128, 1024], F32)
    kmsum = sb("kmsum", [128, 64], F32)
    kmf = sb("kmf", [128, 64], F32)
    kmhi = sb("kmhi", [128, 64], BF16)
    kmlo = sb("kmlo", [128, 64], BF16)
    kmt = sb("kmt", [128, 64], F32)

    qsT = sb("qsT_sb", [128, NH, 4], BF16)
    OTs = sb("OTs_sb", [128, 32], BF16)
    pt128 = sb("pt128_sb", [128, 1], U32)
    ARENA = 96000
    arena = sb("arena", [128, ARENA // 4], F32)

    def carve_any(name, off, shape, dt):
        return carve(name, off, shape, dt)

    def carve(name, off, shape, dt):
        esz = 2 if dt == BF16 else 4
        n = 1
        for d_ in shape[1:]:
            n *= d_
        nbytes = n * esz
        assert off % 4 == 0 and off + nbytes <= ARENA, (name, off, nbytes)
        P.region(name, off, off + nbytes)
        v = arena[0:shape[0], off // 4: (off + nbytes + 3) // 4]
        if dt != F32:
            v = v.bitcast(dt)[:, 0:n]
        if len(shape) == 3:
            v = v.rearrange("p (a b) -> p a b", a=shape[1])
        elif len(shape) == 4:
            v = v.rearrange("p (a b c) -> p a b c", a=shape[1], b=shape[2])
        return v

    xt = carve("xt", 0, [128, 2, D], F32)
    hT1 = carve("hT1", 16384, [128, NKC, 516], BF16)
    ropeC1 = carve("ropeC1", 32896, [32, 516], F32)
    ropeS1 = carve("ropeS1", 34960, [32, 516], F32)
    kst = carve("kst", 37024, [128, 516], F32)
    kb32 = carve("kb32", 39088, [32, 516], BF16)
    t1 = carve("t1", 40120, [32, 516], F32)
    t2 = carve("t2", 42184, [32, 516], F32)
    ktok = carve("ktok", 44248, [128, 4, 128], F32)
    vtok = carve("vtok", 46296, [128, 2, WB], F32)
    wsp = carve("wsp", 48344, [128, 8, 128], F32)

    H = {}
    H["gbuf"] = [carve("gbuf0", 52440, [128, 4096], F32), carve("gbuf1", 68824, [128, 4096], F32)]
    P.region("gbuf", 52440, 85208)
    H["acc2"] = carve("acc2", 85208, [128, 2048], F32)
    qtok = carve("qtok", 0, [4, 1024], F32)
    H["qbc"] = carve("qbc", 4096, [128, 4096], F32)
    o_ = 52440
    for n_, shp_, dt_ in [("kms_sb", [64, 1024], F32), ("kmsf", [128, 8, 64], F32), ("kmst", [128, 8, 64], F32),
                          ("kmshi", [128, 8, 64], BF16), ("kmslo", [128, 8, 64], BF16), ("gs", [4, 512], F32),
                          ("top8s", [4, 64], F32), ("idx8", [4, 64], U32), ("idx3c", [4, 24], U32), ("nb", [128, 1], U32),
                          ("pg", [128, 2], I32), ("pgf", [128, 2], F32), ("rb96", [128, 2], F32), ("rbb", [128, 192], F32),
                          ("ridx", [128, 192], U32), ("Pown", [4, 32], F32), ("Pownb", [4, 32], BF16), ("Kg", [128, 8, 128], F32),
                          ("Vg", [128, 12, 128], F32), ("tmpk", [128, 128], F32), ("S", [128, 192], F32), ("Pm", [128, 192], F32),
                          ("Pr", [128, 32], F32), ("rden", [128, 32], F32), ("pair", [128, 64], F32), ("iota", [128, 192], F32),
                          ("cm", [4, 32], F32), ("ones_f", [128, 128], F32)]:
        esz_ = 2 if dt_ == BF16 else 4
        nb_ = esz_
        for d_ in shp_[1:]:
            nb_ *= d_
        nb_ = (nb_ + 3) // 4 * 4
        H[n_] = carve_any(n_, o_, shp_, dt_)
        o_ += nb_
    assert o_ <= 85208, o_
    H.update(cache_k_pages=cache_k.rearrange("(pg pos) c -> pg (pos c)", pos=128), cache_k_rows=cache_k, cache_v_rows=cache_v,
             ptp=ptp_d, s_idx=s_idx, s_pg=s_pg, pt128=pt128, ident_f=ident_f, qsT=qsT, kT=kT, Vs=Vs, OTs=OTs)

    NG = 260
    xres = carve("xres", 0, [128, 3, D], F32)
    hT = carve("hT", 24576, [128, NKC, NG], BF16)
    XB = 32896
    mix = carve("mix", XB, [128, 3, D], F32)
    qT = carve("qT", XB, [128, NH, NG], BF16)
    uT = carve("uT", XB + 4160, [128, NH, NG], BF16)
    gaT = carve("gaT", XB + 8320, [128, NKC, NG], BF16)
    YB = 57472
    actT = carve("actT", YB, [128, 44, NG], BF16)
    sa = carve("sa", YB + 22880, [128, 2, NG], F32)
    mixgT = carve("mixgT", YB, [128, NKC, NG], BF16)
    gf = carve("gf", YB + 8320, [128, 3, 1024], F32)
    PT = carve("PT", YB + 8320, [128, 16, 256], BF16)
    gbT = carve("gbT", YB + 8320, [128, NKC, NG], BF16)
    ZB = 82432
    vn = carve("vn", ZB, [128, 3, 1024], BF16)
    ropeC2 = carve("ropeC2", ZB + 6144, [32, NG], F32)
    ropeS2 = carve("ropeS2", ZB + 7184, [32, NG], F32)
    qst = carve("qst", ZB + 8224, [128, NG], F32)
    qb32 = carve("qb32", ZB + 9264, [32, NG], BF16)
    q1 = carve("q1", ZB + 9784, [32, NG], F32)
    q2 = carve("q2", ZB + 10824, [32, NG], F32)
    acc = carve("acc", ZB + 6144, [128, 2, 130], F32)
    Otok = carve("Otok", ZB + 7184, [128, 2, 1024], BF16)
    gm = carve("gm", ZB + 11280, [128, 2, 64], F32)
    top8 = carve("top8", ZB + 11792, [128, 2, 64], F32)
    thr = carve("thr", ZB + 12304, [128, 2, 8], F32)
    sel = carve("sel", ZB + 12368, [128, 2, 64], F32)
    rsum = carve("rsum", ZB + 12880, [128, 4], F32)
    ta = carve("ta", ZB, [128, 2, NG], F32)
    tb = carve("tb", ZB + 2080, [128, 2, NG], F32)
    pst = carve("pst", ZB, [128, 256], F32)
    pbt = carve("pbt", ZB + 1024, [128, 256], BF16)
    pT = carve("pT", ZB + 1536, [128, 2, NG], BF16)
    tg = carve("tg", ZB + 2576, [128, WB], F32)

    psA = st.enter_context(nc.psum_tensor("psA", [128, 6, 512], F32))
    psT = st.enter_context(nc.psum_tensor("psT", [128, 2, 1024], BF16))

    cnt = {"ps": 0, "stat": 0, "xt": 0, "ktok": 0, "vtok": 0, "acc": 0}

    def bank():
        b = cnt["ps"] % 6
        cnt["ps"] += 1
        return b

    def statcol():
        sc = (cnt["stat"] % 32) * 8
        cnt["stat"] += 1
        return sc

    def dma(eng, out, in_, r, w, key):
        return P.op(eng, lambda e: e.dma_start(out=out, in_=in_), r=r, w=w, dma=key)

    def ACT(out, in_, func, r, w, **kw):
        return P.op("act", lambda e: e.activation(out=out, in_=in_, func=func, **kw), r, w)

    def TT(eng, out, in0, in1, op, r, w):
        return P.op(eng, lambda e: e.tensor_tensor(out=out, in0=in0, in1=in1, op=op), r, w)

    def TS(eng, out, in0, s1, s2, op0, op1, r, w):
        if s2 is None:
            return P.op(eng, lambda e: e.tensor_scalar(out=out, in0=in0, scalar1=s1, scalar2=None, op0=op0), r, w)
        return P.op(eng, lambda e: e.tensor_scalar(out=out, in0=in0, scalar1=s1, scalar2=s2, op0=op0, op1=op1), r, w)

    def STT(eng, out, in0, scalar, in1, op0, op1, r, w):
        return P.op(eng, lambda e: e.scalar_tensor_tensor(out=out, in0=in0, scalar=scalar, in1=in1, op0=op0, op1=op1), r, w)

    def COPY(eng, out, in_, r, w):
        if eng == "act":
            return ACT(out, in_, AF.Copy, r, w)
        return P.op(eng, lambda e: e.tensor_copy(out=out, in_=in_), r, w)

    def MM(out, lhsT, rhs, start, stop, r, w):
        return P.op("pe", lambda e: e.matmul(out, lhsT=lhsT, rhs=rhs, start=start, stop=stop), r, w)

    def TR(out, in_, ident, r, w):
        return P.op("pe", lambda e: e.transpose(out, in_, ident), r, w)

    def RECIP(out, in_, r, w):
        return P.op("dve", lambda e: e.reciprocal(out=out, in_=in_), r, w)

    wjobs = []

    def wblock(w, r0, nkc, c0):
        wjobs.append((w[r0:r0 + nkc * 128, c0:c0 + WB], nkc))

    for g1 in range(4):
        for blk in range(4):
            wblock(w_in, 0, 16, 1024 + blk * WB)
        for blk in range(4):
            wblock(w_in, 0, 16, 2048 + blk * WB)
    for blk in range(4):
        wblock(w_in, 0, 16, blk * WB)
    for gi in range(4):
        for blk in range(4):
            wblock(w_in, 0, 16, blk * WB)
        for blk in range(4):
            wblock(w_in, 0, 16, 3072 + blk * WB)
        for blk in range(4):
            wblock(w_in, 0, 16, 4096 + blk * WB)
        for blk in range(16):
            wblock(w_gate, 0, 16, blk * WB)
        for blk in range(8):
            wblock(w_a_out, 0, 8, blk * WB)
            wblock(w_b_out, 0, 8, blk * WB)
        for blk in range(8):
            wblock(w_o, 0, 16, blk * WB)
        for j in range(22):
            wblock(w_ffn_in, 0, 16, j * WB)
            wblock(w_ffn_in, 0, 16, FFN + j * WB)
        for cb in range(8):
            wblock(w_ffn_out, 0, 16, cb * WB)
            wblock(w_ffn_out, 2048, 16, cb * WB)
            wblock(w_ffn_out, 4096, 12, cb * WB)
        for blk in range(8):
            wblock(w_ple_gate, 0, 16, blk * WB)
        for blk in range(8):
            wblock(w_ple, 0, 2, blk * WB)
    wstate = {"issued": 0, "used": 0}

    def w_next(expect_nkc):
        while wstate["issued"] < min(len(wjobs), wstate["used"] + NS):
            j = wstate["issued"]
            ap, nkc = wjobs[j]
            s = j % NS
            dma("pool", wbuf[:, s, 0:nkc, :], ap.rearrange("(kc p) c -> p kc c", p=128), [], [("w", s)], ("w", s))
            wstate["issued"] += 1
        j = wstate["used"]
        assert wjobs[j][1] == expect_nkc, (j, wjobs[j][1], expect_nkc)
        wstate["used"] += 1
        return j % NS

    dma("sp", ident_bf[:], c_ident_bf, [], ["ident_bf"], "c0")
    dma("sp", ident_f[:], c_ident_f, [], ["ident_f"], "c1")
    dma("sp", perm[:], c_perm, [], ["perm"], "c2")
    dma("sp", tri_bf[:], c_tri_bf, [], ["tri_bf"], "c3")
    dma("sp", tri_f[:], c_tri_f, [], ["tri_f"], "c4")
    dma("sp", negm[:], c_negm.partition_broadcast(128), [], ["negm"], "c5")
    dma("sp", bsb[:], b_spatial.partition_broadcast(128), [], ["bsb"], "c6")
    P.op("pool", lambda e: e.memset(Vaug[:, :, :, 128:130], 1.0), [], [("Vaug", k) for k in range(16)])
    P.op("pool", lambda e: e.memset(Vs[:, :, 128:130], 1.0), [], ["Vs"])
    P.op("pool", lambda e: e.memset(kmsum[:], 0.0), [], ["kmsum"])

    def load_g(ap, width=D):
        dma("sp", gbc[:, 0:width], ap.partition_broadcast(128), [], ["gbc"], "gbc")

    dma("sp", wsp[:], w_spatial.rearrange("g t s -> t g s"), [], ["wsp"], "c7")
    for g in range(8):
        b = bank()
        TR(psA[:, b, 0:128], wsp[:, g, :], ident_f[:, :], ["wsp", "ident_f"], [("ps", b)])
        TT("dve", WT[:, g, :], psA[:, b, 0:128], tri_f[:, :], ALU.mult, [("ps", b), "tri_f"], [("WT", g), ("ps", b)])
    WTres = [("WT", g) for g in range(8)]

    def rstd_of(src, M, src_res, width):
        sc = statcol()
        sres = ("stat", sc)
        ACT(junk[0:M, 0:width], src, AF.Square, src_res, ["hbt", sres], accum_out=stat[0:M, sc:sc + 1])
        TS("dve", stat[0:M, sc + 1:sc + 2], stat[0:M, sc:sc + 1], 1.0 / width, EPS, ALU.mult, ALU.add, [sres], [sres])
        ACT(stat[0:M, sc + 2:sc + 3], stat[0:M, sc + 1:sc + 2], AF.Sqrt, [sres], [sres])
        RECIP(stat[0:M, sc + 3:sc + 4], stat[0:M, sc + 2:sc + 3], [sres], [sres])
        return stat[0:M, sc + 3:sc + 4], sres

    def norm_to_T(src, M, src_res, dstT, tok0, dst_res):
        rs, sres = rstd_of(src, M, src_res, D)
        STT("dve", hbt[0:M, :], src, rs, gbc[0:M, :], ALU.mult, ALU.mult, src_res + [sres, "gbc"], ["hbt"])
        for b in range(2):
            for k in range(8):
                kc = b * 8 + k
                TR(psT[:, b, k * 128:k * 128 + M], hbt[0:M, kc * 128:(kc + 1) * 128], ident_bf[0:M, 0:M],
                   ["hbt", "ident_bf"], [("psT", b)])
            src_ps = psT[:, b, :].rearrange("p (k t) -> p k t", k=8)[:, :, 0:M]
            COPY("act" if b == 0 else "dve", dstT[:, b * 8:(b + 1) * 8, tok0:tok0 + M], src_ps, [("psT", b)], [dst_res, ("psT", b)])

    def rope_evac(b, n, stg, stg_res, b32, b32_res, ta_, ta_res, tb_, tb_res, rC, rS, c0):
        ACT(stg[:, c0:c0 + n], psA[:, b, 0:n], AF.Copy, [("ps", b)], [stg_res, ("ps", b)])
        COPY("dve", b32[:, c0:c0 + n], stg[0:32, c0:c0 + n], [stg_res], [b32_res])
        b2 = bank()
        MM(psA[0:32, b2, 0:n], perm[:, :], b32[:, c0:c0 + n], True, True, [b32_res, "perm"], [("ps", b2)])
        TT("dve", ta_[:, c0:c0 + n], stg[0:32, c0:c0 + n], rC[:, c0:c0 + n], ALU.mult, [stg_res, "rope"], [ta_res])
        TT("dve", tb_[:, c0:c0 + n], psA[0:32, b2, 0:n], rS[:, c0:c0 + n], ALU.mult, [("ps", b2), "rope"], [tb_res, ("ps", b2)])
        TT("dve", stg[0:32, c0:c0 + n], ta_[:, c0:c0 + n], tb_[:, c0:c0 + n], ALU.add, [ta_res, tb_res], [stg_res])

    H["psA"] = psA
    dma("sp", pt128[:], pt128_d, [], ["pt128"], "c8")
    scan = {"c": 0}

    def scan_step():
        if scan["c"] < 32:
            emit_sample_scan_chunk(P, nc, scan["c"], H)
            scan["c"] += 1

    load_g(gvecs["g_pre_mix"])
    for g1 in range(4):
        prev = g1 < 2
        base_tok = (g1 % 2) * 512
        last = g1 == 3
        kbase = g1 * 512
        tiles = [(t * 128, 128) for t in range(4)] + ([(512, 4)] if last else [])
        for ti, (tk, M) in enumerate(tiles):
            if M == 128:
                src = (x_prev if prev else x_own)[base_tok + tk: base_tok + tk + 128, :]
            else:
                src = x_own[1024:1028, :]
            xs = cnt["xt"] % 2
            cnt["xt"] += 1
            dma("sp", xt[0:M, xs, :], src, [], [("xt", xs)], ("xt", xs))
            norm_to_T(xt[0:M, xs, :], M, [("xt", xs)], hT1, tk, ("hT1", ti))
        hres = [("hT1", t) for t in range(len(tiles))]
        rc = (ropeC_prev if prev else ropeC_own)
        rs_ = (ropeS_prev if prev else ropeS_own)
        dma("sp", ropeC1[:, 0:512], rc[:, base_tok:base_tok + 512], [], ["rope"], "rc")
        dma("sp", ropeS1[:, 0:512], rs_[:, base_tok:base_tok + 512], [], ["rope"], "rc")
        if last:
            dma("sp", ropeC1[:, 512:516], ropeC_own[:, 1024:1028], [], ["rope"], "rc")
            dma("sp", ropeS1[:, 512:516], ropeS_own[:, 1024:1028], [], ["rope"], "rc")
        chunks = [(0, 512)] + ([(512, 4)] if last else [])
        for blk in range(4):
            s = w_next(16)
            for j in range(2):
                h = blk * 2 + j
                for (c0, n) in chunks:
                    b = bank()
                    for kc in range(NKC):
                        MM(psA[:, b, 0:n], wbuf[:, s, kc, j * 128:(j + 1) * 128], hT1[:, kc, c0:c0 + n], kc == 0, kc == NKC - 1,
                           [("w", s)] + hres, [("ps", b)])
                    rope_evac(b, n, kst, "kst", kb32, "kb32", t1, "t1", t2, "t2", ropeC1, ropeS1, c0)
                    kofs = kbase + c0 if c0 == 0 else 2048
                    ACT(kT[:, h, kofs:kofs + n], kst[:, c0:c0 + n], AF.Copy, ["kst"], [("kT", h, g1, c0)])
                    if c0 == 0:
                        P.op("dve", lambda e, h=h, g1=g1: e.tensor_reduce(
                            out=kmsum[:, h * 8 + g1 * 2: h * 8 + g1 * 2 + 2],
                            in_=kst[:, 0:512].rearrange("p (b k) -> p b k", b=2), axis=AX.X, op=ALU.add),
                             ["kst"], [("kmsum", h, g1)])
                    if not prev:
                        ttiles = [(c0 + i * 128, 128) for i in range(4)] if n == 512 else [(c0, 4)]
                        for (tk, M) in ttiles:
                            b3 = bank()
                            TR(psA[0:M, b3, 0:128], kst[:, tk:tk + M], ident_f[:, :], ["kst", "ident_f"], [("ps", b3)])
                            ks = cnt["ktok"] % 4
                            cnt["ktok"] += 1
                            COPY("dve", ktok[0:M, ks, :], psA[0:M, b3, 0:128], [("ps", b3)], [("ktok", ks), ("ps", b3)])
                            row0 = (base_tok + tk) if n == 512 else 1024
                            outs.append(dma("sp", o_k[row0:row0 + M, h * 128:(h + 1) * 128], ktok[0:M, ks, :],
                                            [("ktok", ks)], [], ("ktok", ks)))
            scan_step()
        for blk in range(4):
            s = w_next(16)
            h0 = blk * 2
            for (tk, M) in tiles:
                b = bank()
                for kc in range(NKC):
                    MM(psA[0:M, b, 0:WB], hT1[:, kc, tk:tk + M], wbuf[:, s, kc, :], kc == 0, kc == NKC - 1,
                       [("w", s)] + hres, [("ps", b)])
                src = psA[0:M, b, 0:WB].rearrange("p (h d) -> p h d", h=2)
                if M == 128:
                    kt = (kbase + tk) // 128
                    ACT(Vaug[:, kt, h0:h0 + 2, 0:128], src, AF.Copy, [("ps", b)], [("Vaug", kt), ("ps", b)])
                else:
                    ACT(Vs[:, h0:h0 + 2, 0:128], src, AF.Copy, [("ps", b)], ["Vs", ("ps", b)])
                if not prev:
                    vs_ = cnt["vtok"] % 2
                    cnt["vtok"] += 1
                    COPY("dve", vtok[0:M, vs_, :], psA[0:M, b, 0:WB], [("ps", b)], [("vtok", vs_), ("ps", b)])
                    row0 = (base_tok + tk) if M == 128 else 1024
                    outs.append(dma("sp", o_v[row0:row0 + M, h0 * 128:h0 * 128 + WB], vtok[0:M, vs_, :],
                                    [("vtok", vs_)], [], ("vtok", vs_)))
            scan_step()

    for blk in range(4):
        s = w_next(16)
        for j in range(2):
            h = blk * 2 + j
            b = bank()
            for kc in range(NKC):
                MM(psA[:, b, 0:4], wbuf[:, s, kc, j * 128:(j + 1) * 128], hT1[:, kc, 512:516], kc == 0, kc == NKC - 1,
                   [("w", s)] + hres, [("ps", b)])
            rope_evac(b, 4, kst, "kst", kb32, "kb32", t1, "t1", t2, "t2", ropeC1, ropeS1, 512)
            ACT(qsT[:, h, 0:4], kst[:, 512:516], AF.Copy, ["kst"], ["qsT"])
            b3 = bank()
            TR(psA[0:4, b3, 0:128], kst[:, 512:516], ident_f[:, :], ["kst", "ident_f"], [("ps", b3)])
            COPY("dve", qtok[0:4, h * 128:(h + 1) * 128], psA[0:4, b3, 0:128], [("ps", b3)], ["qtok", ("ps", b3)])
    dma("sp", s_q, qtok[0:4, :], ["qtok"], ["s_q"], "s_q")
    dma("sp", H["qbc"][:, :], s_q.rearrange("a (b o) -> o (a b)", o=1).partition_broadcast(128), ["s_q"], ["qbc"], "qbc")
    while scan["c"] < 32:
        scan_step()
    dma("sp", H["pair"][:, :], c_pair, [], ["pair"], "c9")
    dma("sp", H["iota"][:, :], c_iota, [], ["iota"], "c10")
    dma("sp", H["cm"][:, :], c_cm, [], ["cm"], "c11")
    dma("sp", H["ones_f"][:, :], c_ones, [], ["ones_f"], "c12")
    emit_sample_attention(P, nc, H, bank)

    kmres = [("kmsum", h, g1) for h in range(8) for g1 in range(4)] + ["kmsum"]
    TS("dve", kmf[:], kmsum[:], 1.0 / 256, None, ALU.mult, None, kmres, ["kmf"])
    COPY("dve", kmhi[:], kmf[:], ["kmf"], ["kmhi"])
    COPY("dve", kmt[:], kmhi[:], ["kmhi"], ["kmt"])
    TT("dve", kmt[:], kmf[:], kmt[:], ALU.subtract, ["kmf", "kmt"], ["kmt"])
    COPY("dve", kmlo[:], kmt[:], ["kmt"], ["kmlo"])

    SCALE = float(HD) ** -0.5
    for gi in range(4):
        last = gi == 3
        ntok = NG if last else 256
        tok0 = gi * 256
        tiles = [(0, 0, 128), (1, 128, 128)] + ([(2, 256, 4)] if last else [])
        n_own = 4 + gi
        nkt = 2 * n_own + 2

        load_g(gvecs["g_pre_mix"])
        for (ti, tk, M) in tiles:
            r0 = tok0 + tk if M == 128 else 1024
            dma("sp", xres[0:M, ti, :], x_own[r0:r0 + M, :], [], [("xres", ti)], ("xres", ti))
            norm_to_T(xres[0:M, ti, :], M, [("xres", ti)], hT, tk, ("hT", ti))
        hres = [("hT", ti) for (ti, _, _) in tiles]
        dma("sp", ropeC2[:, 0:256], ropeC_own[:, tok0:tok0 + 256], [], ["rope"], "rc")
        dma("sp", ropeS2[:, 0:256], ropeS_own[:, tok0:tok0 + 256], [], ["rope"], "rc")
        if last:
            dma("sp", ropeC2[:, 256:260], ropeC_own[:, 1024:1028], [], ["rope"], "rc")
            dma("sp", ropeS2[:, 256:260], ropeS_own[:, 1024:1028], [], ["rope"], "rc")

        for blk in range(4):
            s = w_next(16)
            for j in range(2):
                h = blk * 2 + j
                b = bank()
                for kc in range(NKC):
                    MM(psA[:, b, 0:ntok], wbuf[:, s, kc, j * 128:(j + 1) * 128], hT[:, kc, 0:ntok], kc == 0, kc == NKC - 1,
                       [("w", s)] + hres, [("ps", b)])
                rope_evac(b, ntok, qst, "qst", qb32, "qb32", q1, "q1", q2, "q2", ropeC2, ropeS2, 0)
                ACT(qT[:, h, 0:ntok], qst[:, 0:ntok], AF.Copy, ["qst"], [("qT", h)])
        for blk in range(4):
            s = w_next(16)
            for j in range(2):
                c = blk * 2 + j
                b = bank()
                for kc in range(NKC):
                    MM(psA[:, b, 0:ntok], wbuf[:, s, kc, j * 128:(j + 1) * 128], hT[:, kc, 0:ntok], kc == 0, kc == NKC - 1,
                       [("w", s)] + hres, [("ps", b)])
                ACT(uT[:, c, 0:ntok], psA[:, b, 0:ntok], AF.Gelu_apprx_tanh, [("ps", b)], [("uT", c), ("ps", b)])
        for blk in range(4):
            s = w_next(16)
            for (ti, tk, M) in tiles:
                b = bank()
                for kc in range(NKC):
                    MM(psA[0:M, b, 0:WB], hT[:, kc, tk:tk + M], wbuf[:, s, kc, :], kc == 0, kc == NKC - 1,
                       [("w", s)] + hres, [("ps", b)])
                ACT(gf[0:M, ti, blk * WB:(blk + 1) * WB], psA[0:M, b, 0:WB], AF.Gelu_apprx_tanh, [("ps", b)],
                    [("gf", ti, blk), ("ps", b)])
        load_g(g_vnorm, 1024)
        for (ti, tk, M) in tiles:
            gres = [("gf", ti, blk) for blk in range(4)]
            sc = statcol()
            sres = ("stat", sc)
            ACT(junk[0:M, 0:1024], gf[0:M, ti, :], AF.Copy, gres, ["hbt", sres], accum_out=stat[0:M, sc:sc + 1])
            ACT(junk[0:M, 0:1024], gf[0:M, ti, :], AF.Square, gres, ["hbt", sres], accum_out=stat[0:M, sc + 1:sc + 2])
            TS("dve", stat[0:M, sc + 2:sc + 3], stat[0:M, sc:sc + 1], 1.0 / 1024, None, ALU.mult, None, [sres], [sres])
            TT("dve", stat[0:M, sc + 3:sc + 4], stat[0:M, sc + 2:sc + 3], stat[0:M, sc + 2:sc + 3], ALU.mult, [sres], [sres])
            TS("dve", stat[0:M, sc + 4:sc + 5], stat[0:M, sc + 1:sc + 2], 1.0 / 1024, EPS, ALU.mult, ALU.add, [sres], [sres])
            TT("dve", stat[0:M, sc + 4:sc + 5], stat[0:M, sc + 4:sc + 5], stat[0:M, sc + 3:sc + 4], ALU.subtract, [sres], [sres])
            ACT(stat[0:M, sc + 5:sc + 6], stat[0:M, sc + 4:sc + 5], AF.Sqrt, [sres], [sres])
            RECIP(stat[0:M, sc + 6:sc + 7], stat[0:M, sc + 5:sc + 6], [sres], [sres])
            TS("dve", gf[0:M, ti, :], gf[0:M, ti, :], stat[0:M, sc + 2:sc + 3], stat[0:M, sc + 6:sc + 7], ALU.subtract, ALU.mult,
               gres + [sres], gres)
            if M == 4:
                TT("dve", gf[0:M, ti, :], gf[0:M, ti, :], gbc[0:M, 0:1024], ALU.mult, gres + ["gbc"], gres)
                outs.append(dma("sp", o_gv[:, :], gf[0:M, ti, :], gres, [], "ogv"))
                COPY("dve", vn[0:M, ti, :], gf[0:M, ti, :], gres, [("vn", ti)])
            else:
                TT("dve", vn[0:M, ti, :], gf[0:M, ti, :], gbc[0:M, 0:1024], ALU.mult, gres + ["gbc"], [("vn", ti)])
        for (ti, tk, M) in tiles:
            for g in range(8):
                b = bank()
                MM(psA[:, b, 0:M], vn[0:M, ti, g * 128:(g + 1) * 128], WT[0:M, g, 0:M], True, True,
                   [("vn", ti), ("WT", g)], [("ps", b)])
                TT("dve", psA[:, b, 0:M], psA[:, b, 0:M], bsb[:, g * 128:g * 128 + M], ALU.add, [("ps", b), "bsb"], [("ps", b)])
                TT("dve", uT[:, g, tk:tk + M], psA[:, b, 0:M], uT[:, g, tk:tk + M], ALU.mult, [("ps", b), ("uT", g)],
                   [("uT", g), ("ps", b)])
        BTres = [("uT", g) for g in range(8)]

        for qt in range(2):
            b = bank()
            for h in range(8):
                MM(psA[:, b, h * 8:(h + 1) * 8], qT[:, h, qt * 128:(qt + 1) * 128], kmhi[:, h * 8:(h + 1) * 8], True, False,
                   [("qT", h), "kmhi"], [("ps", b)])
                MM(psA[:, b, h * 8:(h + 1) * 8], qT[:, h, qt * 128:(qt + 1) * 128], kmlo[:, h * 8:(h + 1) * 8], False, True,
                   [("qT", h), "kmlo"], [("ps", b)])
            gmr = ("gm", qt)
            TT("dve", gm[:, qt, :], psA[:, b, 0:64], negm[:, gi * 64:(gi + 1) * 64], ALU.add, [("ps", b), "negm"], [gmr, ("ps", b)])
            for h in range(8):
                P.op("dve", lambda e, h=h, qt=qt: e.max(out=top8[:, qt, h * 8:(h + 1) * 8], in_=gm[:, qt, h * 8:(h + 1) * 8]),
                     [gmr], [gmr])
            TS("dve", thr[:, qt, :], top8[:, qt, :].rearrange("p (h k) -> p h k", k=8)[:, :, 2], -1e29, None, ALU.max, None,
               [gmr], [gmr])
            for h in range(8):
                TS("dve", sel[:, qt, h * 8:(h + 1) * 8], gm[:, qt, h * 8:(h + 1) * 8], thr[:, qt, h:h + 1], None, ALU.is_ge, None,
                   [gmr], [gmr])
        ka, kb_ = 2 * n_own, 2 * n_own + 1
        for h in range(8):
            for kt in range(nkt):
                bq = bank()
                MM(psA[:, bq, 0:256], kT[:, h, kt * 128:(kt + 1) * 128], qT[:, h, 0:256], True, True, [("qT", h)], [("ps", bq)])
                ACT(PT[:, kt, :], psA[:, bq, 0:256], AF.Exp, [("ps", bq)], [("PT", kt), ("ps", bq)], scale=SCALE)
            TT("pool", PT[:, ka, 0:128], PT[:, ka, 0:128], tri_bf[:, :], ALU.mult, [("PT", ka), "tri_bf"], [("PT", ka)])
            TT("pool", PT[:, kb_, 128:256], PT[:, kb_, 128:256], tri_bf[:, :], ALU.mult, [("PT", kb_), "tri_bf"], [("PT", kb_)])
            for qt in range(2):
                a = cnt["acc"] % 2
                cnt["acc"] += 1
                ar = ("acc", a)
                gmr = ("gm", qt)
                for n in range(n_own + 1):
                    kts = [2 * n, 2 * n + 1]
                    if n == n_own and qt == 0:
                        kts = [2 * n]
                    b = bank()
                    for i_, kt in enumerate(kts):
                        MM(psA[:, b, 0:129], PT[:, kt, qt * 128:(qt + 1) * 128], Vaug[:, kt, h, 0:129], i_ == 0, i_ == len(kts) - 1,
                           [("PT", kt), ("Vaug", kt)], [("ps", b)])
                    sc_ = sel[:, qt, h * 8 + n:h * 8 + n + 1] if n < n_own else None
                    if n == 0:
                        TS("dve", acc[:, a, 0:129], psA[:, b, 0:129], sc_, None, ALU.mult, None, [("ps", b), gmr], [ar, ("ps", b)])
                    elif n < n_own:
                        STT("dve", acc[:, a, 0:129], psA[:, b, 0:129], sc_, acc[:, a, 0:129], ALU.mult, ALU.add,
                            [("ps", b), gmr, ar], [ar, ("ps", b)])
                    else:
                        TT("dve", acc[:, a, 0:129], psA[:, b, 0:129], acc[:, a, 0:129], ALU.add, [("ps", b), ar], [ar, ("ps", b)])
                RECIP(rsum[:, a:a + 1], acc[:, a, 128:129], [ar], [ar])
                TS("dve", Otok[:, qt, h * 128:(h + 1) * 128], acc[:, a, 0:128], rsum[:, a:a + 1], None, ALU.mult, None, [ar],
                   [("Otok", qt, h)])
        for qt in range(2):
            for h in range(8):
                TR(psT[:, qt, h * 128:(h + 1) * 128], Otok[:, qt, h * 128:(h + 1) * 128], ident_bf[:, :],
                   [("Otok", qt, h), "ident_bf"], [("psT", qt)])
            COPY("act", qT[:, :, qt * 128:(qt + 1) * 128], psT[:, qt, :].rearrange("p (k t) -> p k t", k=8), [("psT", qt)],
                 [("qT", h) for h in range(8)] + [("psT", qt)])
        if last:
            COPY("dve", qT[:, :, 256:260], OTs[:, :].rearrange("p (q h) -> p h q", h=8), ["OTs"], [("qT", h) for h in range(8)])
        OTres = [("qT", h) for h in range(8)]

        for blk in range(16):
            s = w_next(16)
            dst = gaT if blk < 8 else gbT
            dn = "gaT" if blk < 8 else "gbT"
            for j in range(2):
                c = (blk % 8) * 2 + j
                b = bank()
                for kc in range(NKC):
                    MM(psA[:, b, 0:ntok], wbuf[:, s, kc, j * 128:(j + 1) * 128], hT[:, kc, 0:ntok], kc == 0, kc == NKC - 1,
                       [("w", s)] + hres, [("ps", b)])
                ACT(dst[:, c, 0:ntok], psA[:, b, 0:ntok], AF.Sigmoid, [("ps", b)], [(dn, c), ("ps", b)])
        for blk in range(8):
            sa_ = w_next(8)
            for j in range(2):
                c = blk * 2 + j
                b = bank()
                for kc in range(8):
                    MM(psA[:, b, 0:ntok], wbuf[:, sa_, kc, j * 128:(j + 1) * 128], qT[:, kc, 0:ntok], kc == 0, kc == 7,
                       [("w", sa_)] + OTres, [("ps", b)])
                TT("dve", ta[:, j, 0:ntok], psA[:, b, 0:ntok], gaT[:, c, 0:ntok], ALU.mult, [("ps", b), ("gaT", c)], [("ta", j), ("ps", b)])
            sb_ = w_next(8)
            for j in range(2):
                c = blk * 2 + j
                b = bank()
                for kc in range(8):
                    MM(psA[:, b, 0:ntok], wbuf[:, sb_, kc, j * 128:(j + 1) * 128], uT[:, kc, 0:ntok], kc == 0, kc == 7,
                       [("w", sb_)] + BTres, [("ps", b)])
                TT("dve", tb[:, j, 0:ntok], psA[:, b, 0:ntok], gbT[:, c, 0:ntok], ALU.mult, [("ps", b), ("gbT", c)], [("tb", j), ("ps", b)])
                TT("pool", mixgT[:, c, 0:ntok], ta[:, j, 0:ntok], tb[:, j, 0:ntok], ALU.add, [("ta", j), ("tb", j)], [("mixgT", c)])
        mres = [("mixgT", c) for c in range(16)]

        def resid_phase(wname, nblk_w, lhs_buf, lhs_res, gpost, gnext, sub_rows=None):
            pass

        for blk in range(8):
            s = w_next(16)
            for (ti, tk, M) in tiles:
                b = bank()
                for kc in range(NKC):
                    MM(psA[0:M, b, 0:WB], mixgT[:, kc, tk:tk + M], wbuf[:, s, kc, :], kc == 0, kc == NKC - 1,
                       [("w", s)] + mres, [("ps", b)])
                ACT(mix[0:M, ti, blk * WB:(blk + 1) * WB], psA[0:M, b, 0:WB], AF.Copy, [("ps", b)], [("mix", ti, blk), ("ps", b)])

        def post_norm_residual(gname_post, gname_next, dstT):
            load_g(gvecs[gname_post])
            for (ti, tk, M) in tiles:
                mr = [("mix", ti, blk) for blk in range(8)]
                rs, sres = rstd_of(mix[0:M, ti, :], M, mr, D)
                STT("dve", mix[0:M, ti, :], mix[0:M, ti, :], rs, gbc[0:M, :], ALU.mult, ALU.mult, mr + [sres, "gbc"], mr)
                TT("pool", xres[0:M, ti, :], xres[0:M, ti, :], mix[0:M, ti, :], ALU.add, mr + [("xres", ti)], [("xres", ti)])
            load_g(gvecs[gname_next])
            for (ti, tk, M) in tiles:
                norm_to_T(xres[0:M, ti, :], M, [("xres", ti)], dstT, tk, ("hT", ti))

        post_norm_residual("g_post_mix", "g_pre_ffn", hT)

        for j2 in range(22):
            s1 = w_next(16)
            for j in range(2):
                b = bank()
                for kc in range(NKC):
                    MM(psA[:, b, 0:ntok], wbuf[:, s1, kc, j * 128:(j + 1) * 128], hT[:, kc, 0:ntok], kc == 0, kc == NKC - 1,
                       [("w", s1)] + hres, [("ps", b)])
                ACT(sa[:, j, 0:ntok], psA[:, b, 0:ntok], AF.Silu, [("ps", b)], [("sa", j), ("ps", b)])
            s2 = w_next(16)
            for j in range(2):
                c = j2 * 2 + j
                b = bank()
                for kc in range(NKC):
                    MM(psA[:, b, 0:ntok], wbuf[:, s2, kc, j * 128:(j + 1) * 128], hT[:, kc, 0:ntok], kc == 0, kc == NKC - 1,
                       [("w", s2)] + hres, [("ps", b)])
                TT("dve", actT[:, c, 0:ntok], psA[:, b, 0:ntok], sa[:, j, 0:ntok], ALU.mult, [("ps", b), ("sa", j)],
                   [("actT", c), ("ps", b)])
        ares = [("actT", c) for c in range(44)]

        for cb in range(8):
            banks = [bank() for _ in tiles]
            for sub in range(3):
                nk = 16 if sub < 2 else 12
                s = w_next(nk)
                for (ti, tk, M) in tiles:
                    b = banks[ti]
                    for kc in range(nk):
                        MM(psA[0:M, b, 0:WB], actT[:, sub * 16 + kc, tk:tk + M], wbuf[:, s, kc, :], sub == 0 and kc == 0,
                           sub == 2 and kc == nk - 1, [("w", s)] + ares, [("ps", b)])
            for (ti, tk, M) in tiles:
                b = banks[ti]
                ACT(mix[0:M, ti, cb * WB:(cb + 1) * WB], psA[0:M, b, 0:WB], AF.Copy, [("ps", b)], [("mix", ti, cb), ("ps", b)])
        post_norm_residual("g_post_ffn", "g_ple", hT)

        for (ti, tk, M) in tiles:
            r0 = tok0 + tk if M == 128 else 1024
            dma("sp", pst[0:M, :], p_own[r0:r0 + M, :], [], ["pst"], "pst")
            COPY("dve", pbt[0:M, :], pst[0:M, :], ["pst"], ["pbt"])
            for kc in range(2):
                TR(psT[:, 0, kc * 128:kc * 128 + M], pbt[0:M, kc * 128:(kc + 1) * 128], ident_bf[0:M, 0:M], ["pbt", "ident_bf"],
                   [("psT", 0)])
            COPY("act", pT[:, :, tk:tk + M], psT[:, 0, 0:256].rearrange("p (k t) -> p k t", k=2)[:, :, 0:M], [("psT", 0)],
                 [("pT", ti), ("psT", 0)])
        pres = [("pT", ti) for (ti, _, _) in tiles]
        for blk in range(8):
            s = w_next(16)
            for (ti, tk, M) in tiles:
                b = bank()
                for kc in range(NKC):
                    MM(psA[0:M, b, 0:WB], hT[:, kc, tk:tk + M], wbuf[:, s, kc, :], kc == 0, kc == NKC - 1,
                       [("w", s)] + hres, [("ps", b)])
                ACT(mix[0:M, ti, blk * WB:(blk + 1) * WB], psA[0:M, b, 0:WB], AF.Sigmoid, [("ps", b)], [("mix", ti, blk), ("ps", b)])
        for blk in range(8):
            s = w_next(2)
            for (ti, tk, M) in tiles:
                b = bank()
                for kc in range(2):
                    MM(psA[0:M, b, 0:WB], pT[:, kc, tk:tk + M], wbuf[:, s, kc, :], kc == 0, kc == 1,
                       [("w", s)] + pres, [("ps", b)])
                TT("dve", mix[0:M, ti, blk * WB:(blk + 1) * WB], psA[0:M, b, 0:WB], mix[0:M, ti, blk * WB:(blk + 1) * WB], ALU.mult,
                   [("ps", b), ("mix", ti, blk)], [("mix", ti, blk), ("ps", b)])
        for (ti, tk, M) in tiles:
            mr = [("mix", ti, blk) for blk in range(8)]
            TT("pool", xres[0:M, ti, :], xres[0:M, ti, :], mix[0:M, ti, :], ALU.add, mr + [("xres", ti)], [("xres", ti)])
            r0 = tok0 + tk if M == 128 else 1024
            outs.append(dma("sp", o_y[r0:r0 + M, :], xres[0:M, ti, :], [("xres", ti)], [], ("oy", ti)))

    assert wstate["used"] == len(wjobs), (wstate, len(wjobs))
    P.emit(outs)
    st.close()
    return nc


_CACHE = {}


def kernel(x_prompt, x_sample, cache_k, cache_v, page_table, p_prompt, p_sample,
           g_pre_mix, w_in, g_vnorm, w_spatial, b_spatial, w_a_out, w_b_out, w_gate, w_o,
           g_post_mix, g_pre_ffn, w_ffn_in, w_ffn_out, g_post_ffn, g_ple, w_ple_gate, w_ple):
    f32 = np.float32
    bf = ml_dtypes.bfloat16
    A = lambda a: np.ascontiguousarray(np.asarray(a, f32))
    x_prompt = A(x_prompt)
    x_sample = A(x_sample)
    p_prompt = A(p_prompt)
    p_sample = A(p_sample)
    if "nc" not in _CACHE:
        _CACHE["nc"] = build()
    nc = _CACHE["nc"]
    ident = np.eye(128, dtype=f32)
    permm = np.zeros((32, 32), f32)
    for i in range(16):
        permm[i + 16, i] = -1.0
        permm[i, i + 16] = 1.0
    tri = (np.arange(128)[:, None] <= np.arange(128)[None, :]).astype(f32)
    shared = dict(
        w_in=A(w_in)[0], w_gate=A(w_gate)[0], w_a_out=A(w_a_out)[0], w_b_out=A(w_b_out)[0], w_o=A(w_o)[0],
        w_ffn_in=A(w_ffn_in)[0], w_ffn_out=A(w_ffn_out)[0], w_ple_gate=A(w_ple_gate)[0], w_ple=A(w_ple)[0],
        w_spatial=A(w_spatial)[0], b_spatial=A(b_spatial).reshape(1, 1024),
        g_pre_mix=A(g_pre_mix), g_post_mix=A(g_post_mix), g_pre_ffn=A(g_pre_ffn), g_post_ffn=A(g_post_ffn),
        g_ple=A(g_ple), g_vnorm=A(g_vnorm),
        c_ident_bf=ident.astype(bf), c_ident_f=ident, c_perm=permm.astype(bf), c_tri_bf=tri.astype(bf), c_tri_f=tri,
    )
    cP, sP = rope_tables(np.arange(1024))
    ck = np.ascontiguousarray(np.asarray(cache_k, f32)).reshape(1280 * 128, 1024)
    cv = np.ascontiguousarray(np.asarray(cache_v, f32)).reshape(1280 * 128, 1024)
    pt = np.asarray(page_table).astype(np.int32)
    pair = np.zeros((128, 64), f32)
    pair[np.arange(128), np.arange(128) // 2] = 1.0 / 256
    iota = np.tile(np.arange(128, dtype=f32)[:, None], (1, 192))
    cm = np.zeros((4, 32), f32)
    for qq in range(4):
        cm[:qq + 1, qq * 8:(qq + 1) * 8] = 1.0
    shared.update(cache_k=ck, cache_v=cv, c_pair=pair, c_iota=iota, c_cm=cm, c_ones=np.ones((128, 128), f32))
    in_maps = []
    for c in range(8):
        b, half = c // 2, c % 2
        own = x_prompt[b, half * 1024:(half + 1) * 1024]
        xo = np.concatenate([own, x_sample[c]], 0)
        xp = x_prompt[b, 0:1024] if half == 1 else np.zeros((1024, D), f32)
        po = np.concatenate([p_prompt[0, b, half * 1024:(half + 1) * 1024], p_sample[0, c]], 0)
        pos_own = np.concatenate([half * 1024 + np.arange(1024), 16384 + np.arange(4)])
        cO, sO = rope_tables(pos_own)
        negm = np.zeros((4, 8, 8), f32)
        for gi in range(4):
            for n in range(8):
                ok = (n < 4 + gi) and (n >= 4 or half == 1)
                negm[gi, :, n] = 0.0 if ok else NEG
        m = dict(shared)
        m.update(x_own=np.ascontiguousarray(xo), x_prev=np.ascontiguousarray(xp), p_own=np.ascontiguousarray(po),
                 ropeC_own=cO, ropeS_own=sO, ropeC_prev=cP, ropeS_prev=sP, c_negm=negm.reshape(1, 256),
                 ptp=np.ascontiguousarray(pt[c].reshape(64, 2)), pt128=np.ascontiguousarray(pt[c].reshape(128, 1).astype(np.uint32)))
        in_maps.append(m)
    res = run_bass_kernel_spmd(nc, in_maps, core_ids=list(range(8)))
    R = res.results
    y_prompt = np.zeros((4, 2048, D), f32)
    y_sample = np.zeros((8, 4, D), f32)
    k_p = np.zeros((1, 4, 2048, NH, HD), f32)
    v_p = np.zeros((1, 4, 2048, NH, HD), f32)
    k_s = np.zeros((1, 8, 4, NH, HD), f32)
    v_s = np.zeros((1, 8, 4, NH, HD), f32)
    gv_s = np.zeros((1, 8, 4, 1024), f32)
    for c in range(8):
        b, half = c // 2, c % 2
        ok = np.asarray(R[c]["o_k"], f32)
        ov = np.asarray(R[c]["o_v"], f32)
        oy = np.asarray(R[c]["o_y"], f32)
        sl = slice(half * 1024, (half + 1) * 1024)
        k_p[0, b, sl] = ok[:1024].reshape(1024, NH, HD)
        v_p[0, b, sl] = ov[:1024].reshape(1024, NH, HD)
        y_prompt[b, sl] = oy[:1024]
        y_sample[c] = oy[1024:]
        k_s[0, c] = ok[1024:].reshape(4, NH, HD)
        v_s[0, c] = ov[1024:].reshape(4, NH, HD)
        gv_s[0, c] = np.asarray(R[c]["o_gv"], f32)
    return (y_prompt, y_sample, k_p, v_p, k_s, v_s, gv_s)
```

```python
import numpy as np
import ml_dtypes
import concourse.bass as bass
import concourse.mybir as mybir
from concourse.bass_utils import run_bass_kernel_spmd

F32 = mybir.dt.float32
F32R = mybir.dt.float32r
BF16 = mybir.dt.bfloat16
I32 = mybir.dt.int32
U32 = mybir.dt.uint32
AF = mybir.ActivationFunctionType
ALU = mybir.AluOpType
AX = mybir.AxisListType

D = 2048
NH = 8
HD = 128
NT = 1028
NTP = 1024
FFN = 5632
NKC = 16
WB = 256
EPS = 1e-6
NEG = -1e30


class Ins:
    __slots__ = ("eng", "fn", "idx", "dma", "deps", "signal", "sem", "val", "key")

    def __init__(self, eng, fn, idx, dma, key):
        self.eng, self.fn, self.idx, self.dma, self.key = eng, fn, idx, dma, key
        self.deps = set()
        self.signal = False
        self.sem = None
        self.val = 0


class Prog:
    ENGS = ("pe", "act", "dve", "pool", "sp")

    def __init__(self, nc):
        self.nc = nc
        self.streams = {e: [] for e in self.ENGS}
        self.lw = {}
        self.rd = {}
        self.dma_keys = {}
        self.bufreg = {}
        self.active = set()
        self.bufres = {}
        self.pending = {}

    def region(self, buf, lo, hi):
        self.bufreg[buf] = (lo, hi)

    def _bufof(self, x):
        b = x[0] if isinstance(x, tuple) else x
        return b if b in self.bufreg else None

    def _compress(self, users):
        best = {}
        out = []
        seen = set()
        for u in users:
            if u.dma:
                if id(u) not in seen:
                    seen.add(id(u))
                    out.append(u)
            else:
                b = best.get(u.eng)
                if b is None or b.idx < u.idx:
                    best[u.eng] = u
        return out + list(best.values())

    def _touch(self, B):
        if B in self.active:
            return
        lo, hi = self.bufreg[B]
        users = list(self.pending.get(B, ()))
        for O in list(self.active):
            olo, ohi = self.bufreg[O]
            if olo < hi and lo < ohi:
                for res in self.bufres.get(O, ()):
                    l = self.lw.pop(res, None)
                    if l is not None:
                        users.append(l)
                    users.extend(self.rd.pop(res, ()))
                users.extend(self.pending.pop(O, ()))
                self.bufres[O] = set()
                self.active.discard(O)
        self.pending[B] = self._compress(users)
        self.active.add(B)

    def op(self, eng, fn, r=(), w=(), dma=None):
        ins = Ins(eng, fn, len(self.streams[eng]), dma is not None, dma)
        deps = ins.deps
        for x in tuple(r) + tuple(w):
            B = self._bufof(x)
            if B is not None:
                self._touch(B)
                self.bufres.setdefault(B, set()).add(x)
                pend = self.pending.get(B)
                if pend:
                    deps.update(pend)
        for x in r:
            l = self.lw.get(x)
            if l is not None:
                deps.add(l)
        for x in w:
            l = self.lw.get(x)
            if l is not None:
                deps.add(l)
            for q in self.rd.get(x, ()):
                deps.add(q)
        deps.discard(ins)
        for x in r:
            self.rd.setdefault(x, []).append(ins)
        for x in w:
            self.lw[x] = ins
            self.rd[x] = []
        self.streams[eng].append(ins)
        return ins

    def emit(self, final_waits):
        nc = self.nc
        for e in self.ENGS:
            for ins in self.streams[e]:
                for d in ins.deps:
                    if d.eng == "pe" and ins.eng == "pe" and not d.dma:
                        continue
                    d.signal = True
        for ins in final_waits:
            ins.signal = True
        import contextlib
        with contextlib.ExitStack() as st:
            esem = {e: st.enter_context(nc.semaphore("sem_" + e)) for e in self.ENGS}
            ksem = {}
            kcnt = {}
            for e in self.ENGS:
                cnt = 0
                for ins in self.streams[e]:
                    if ins.dma:
                        if ins.key not in ksem:
                            ksem[ins.key] = st.enter_context(nc.semaphore("dq_%d" % len(ksem)))
                            kcnt[ins.key] = 0
                        kcnt[ins.key] += 16
                        ins.sem, ins.val = ksem[ins.key], kcnt[ins.key]
                    elif ins.signal:
                        cnt += 1
                        ins.sem, ins.val = esem[e], cnt
            handles = {"pe": "tensor", "act": "scalar", "dve": "vector", "pool": "gpsimd", "sp": "sync"}
            block = st.enter_context(nc.Block())

            def make(e):
                def body(eng):
                    waited = {}
                    for ins in self.streams[e]:
                        need = {}
                        for d in ins.deps:
                            if d.eng == "pe" and e == "pe" and not d.dma:
                                continue
                            k = id(d.sem)
                            if need.get(k, (None, 0))[1] < d.val:
                                need[k] = (d.sem, d.val)
                        for k, (s, v) in need.items():
                            if waited.get(k, 0) >= v:
                                continue
                            eng.wait_ge(s, v)
                            waited[k] = v
                        bi = ins.fn(eng)
                        if ins.dma:
                            bi.then_inc(ins.sem, 16)
                        elif ins.signal:
                            bi.then_inc(ins.sem, 1)
                    if e == "sp":
                        need = {}
                        for d in final_waits:
                            k = id(d.sem)
                            if need.get(k, (None, 0))[1] < d.val:
                                need[k] = (d.sem, d.val)
                        for k, (s, v) in need.items():
                            eng.wait_ge(s, v)
                return body

            for e in self.ENGS:
                getattr(block, handles[e])(make(e))
            st.close()


def rope_tables(pos):
    half = 16
    freqs = np.power(np.float32(500000.0), (-2.0 * np.arange(half, dtype=np.float32) / 32).astype(np.float32)).astype(np.float32)
    ang = pos.astype(np.float32)[None, :] * freqs[:, None]
    c = np.cos(ang).astype(np.float32)
    s = np.sin(ang).astype(np.float32)
    return np.concatenate([c, c], 0), np.concatenate([s, s], 0)


def emit_sample_scan_chunk(P, nc, c, H):
    gb = H["gbuf"][c % 2]
    res = ("gbuf", c % 2)
    acc2 = H["acc2"]
    P.op("pool", lambda e: e.indirect_dma_start(
        out=gb, out_offset=None, in_=H["cache_k_pages"],
        in_offset=bass.IndirectOffsetOnAxis(ap=H["pt128"][:, 0:1], axis=0), element_offset=c * 4096),
        r=["pt128"], w=[res], dma=res)
    if c == 0:
        P.op("dve", lambda e: e.tensor_tensor(out=acc2[:, :], in0=gb[:, 0:2048], in1=gb[:, 2048:4096], op=ALU.add), [res], ["acc2"])
    else:
        P.op("pool", lambda e: e.tensor_tensor(out=gb[:, 0:2048], in0=gb[:, 0:2048], in1=gb[:, 2048:4096], op=ALU.add), [res], [res])
        P.op("dve", lambda e: e.tensor_tensor(out=acc2[:, :], in0=acc2[:, :], in1=gb[:, 0:2048], op=ALU.add), [res, "acc2"], ["acc2"])


def emit_sample_attention(P, nc, H, bank):
    psA = H["psA"]
    SCALE = float(HD) ** -0.5

    def op(eng, fn, r, w):
        return P.op(eng, fn, r, w)

    def COPY(eng, out, in_, r, w):
        if eng == "act":
            return P.op("act", lambda e: e.activation(out=out, in_=in_, func=AF.Copy), r, w)
        return P.op(eng, lambda e: e.tensor_copy(out=out, in_=in_), r, w)

    def TT(eng, out, in0, in1, o, r, w):
        return P.op(eng, lambda e: e.tensor_tensor(out=out, in0=in0, in1=in1, op=o), r, w)

    def MM(out, lhsT, rhs, start, stop, r, w):
        return P.op("pe", lambda e: e.matmul(out, lhsT=lhsT, rhs=rhs, start=start, stop=stop), r, w)

    def dma(eng, out, in_, r, w, key):
        return P.op(eng, lambda e: e.dma_start(out=out, in_=in_), r=r, w=w, dma=key)

    kms_sb, kmsf, kmst, kmshi, kmslo = H["kms_sb"], H["kmsf"], H["kmst"], H["kmshi"], H["kmslo"]
    ident_f = H["ident_f"]
    acc2 = H["acc2"]
    TT("dve", acc2[:, 0:1024], acc2[:, 0:1024], acc2[:, 1024:2048], ALU.add, ["acc2"], ["acc2"])
    for half in range(2):
        b = bank()
        MM(psA[0:64, b, 0:512], H["pair"][:, :], acc2[:, half * 512:(half + 1) * 512], True, True, ["acc2", "pair"], [("ps", b)])
        COPY("act", kms_sb[0:64, half * 512:(half + 1) * 512], psA[0:64, b, :], [("ps", b)], ["kms_sb", ("ps", b)])
    for h in range(8):
        b = bank()
        P.op("pe", lambda e, b=b, h=h: e.transpose(psA[:, b, 0:64], kms_sb[0:64, h * 128:(h + 1) * 128], ident_f[0:64, 0:64]),
             ["kms_sb", "ident_f"], [("ps", b)])
        COPY("dve", kmsf[:, h, :], psA[:, b, 0:64], [("ps", b)], ["kmsf", ("ps", b)])
    COPY("dve", kmshi[:], kmsf[:], ["kmsf"], ["kmshi"])
    COPY("dve", kmst[:], kmshi[:], ["kmshi"], ["kmst"])
    TT("dve", kmst[:], kmsf[:], kmst[:], ALU.subtract, ["kmsf", "kmst"], ["kmst"])
    COPY("dve", kmslo[:], kmst[:], ["kmst"], ["kmslo"])
    if H.get('stop', 99) <= 1:
        return
    qsT = H["qsT"]
    b = bank()
    for h in range(8):
        MM(psA[0:4, b, h * 64:(h + 1) * 64], qsT[:, h, 0:4], kmshi[:, h, :], True, False, ["qsT", "kmshi"], [("ps", b)])
        MM(psA[0:4, b, h * 64:(h + 1) * 64], qsT[:, h, 0:4], kmslo[:, h, :], False, True, ["qsT", "kmslo"], [("ps", b)])
    gs, top8s, idx8, idx3c = H["gs"], H["top8s"], H["idx8"], H["idx3c"]
    COPY("act", gs[0:4, :], psA[0:4, b, 0:512], [("ps", b)], ["gs", ("ps", b)])
    for h in range(8):
        P.op("dve", lambda e, h=h: e.max(out=top8s[0:4, h * 8:(h + 1) * 8], in_=gs[0:4, h * 64:(h + 1) * 64]), ["gs"], ["top8s"])
        P.op("dve", lambda e, h=h: e.max_index(out=idx8[0:4, h * 8:(h + 1) * 8], in_max=top8s[0:4, h * 8:(h + 1) * 8],
                                             in_values=gs[0:4, h * 64:(h + 1) * 64]), ["gs", "top8s"], ["idx8"])
    COPY("dve", idx3c[0:4, :].rearrange("p (h s) -> p h s", s=3), idx8[0:4, :].rearrange("p (h k) -> p h k", k=8)[:, :, 0:3],
         ["idx8"], ["idx3c"])
    if H.get('stop', 99) <= 2:
        return
    nb, pg, rb96, rbb, ridx = H["nb"], H["pg"], H["rb96"], H["rbb"], H["ridx"]
    P.op("dve", lambda e: e.memset(nb[:, :].bitcast(F32), 0.0), [], ["nb"])
    dma("sp", H["s_idx"], idx3c[0:4, :], ["idx3c"], ["s_idx"], "s_idx")
    dma("sp", nb[0:96, :], H["s_idx"].rearrange("q (c o) -> (q c) o", o=1), ["s_idx"], ["nb"], "nb")
    P.op("pool", lambda e: e.indirect_dma_start(
        out=pg[:, :], out_offset=None, in_=H["ptp"], in_offset=bass.IndirectOffsetOnAxis(ap=nb[:, 0:1], axis=0)),
        r=["nb"], w=["pg"], dma="pg")
    pgf = H["pgf"]
    COPY("dve", pgf[:, :], pg[:, :], ["pg"], ["pgf"])
    P.op("dve", lambda e: e.tensor_scalar(out=rb96[:, :], in0=pgf[:, :], scalar1=128.0, scalar2=None, op0=ALU.mult), ["pgf"], ["rb96"])
    dma("sp", H["s_pg"], rb96[0:96, :], ["rb96"], ["s_pg"], "s_pg")
    dma("sp", rbb[:, :], H["s_pg"].rearrange("a (b o) -> o (a b)", o=1).partition_broadcast(128), ["s_pg"], ["rbb"], "rbb")
    TT("dve", rbb[:, :], rbb[:, :], H["iota"][:, :], ALU.add, ["rbb", "iota"], ["rbb"])
    COPY("dve", ridx[:, :], rbb[:, :], ["rbb"], ["ridx"])
    if H.get('stop', 99) <= 3:
        return
    kT, Vs = H["kT"], H["Vs"]
    bo = bank()
    for q in range(4):
        for h in range(8):
            col = q * 8 + h
            MM(psA[0:4, bo, col:col + 1], kT[:, h, 2048:2052], qsT[:, h, q:q + 1], True, True, ["qsT", "kTs"], [("ps", bo)])
    Pown, Pownb = H["Pown"], H["Pownb"]
    P.op("act", lambda e: e.activation(out=Pown[0:4, :], in_=psA[0:4, bo, 0:32], func=AF.Exp, scale=SCALE), [("ps", bo)],
         ["Pown", ("ps", bo)])
    TT("dve", Pown[0:4, :], Pown[0:4, :], H["cm"][0:4, :], ALU.mult, ["Pown", "cm"], ["Pown"])
    COPY("dve", Pownb[0:4, :], Pown[0:4, :], ["Pown"], ["Pownb"])
    if H.get('stop', 99) <= 4:
        return
    Kg, Vg, tmpk, S, Pm, Pr = H["Kg"], H["Vg"], H["tmpk"], H["S"], H["Pm"], H["Pr"]
    qbc = H["qbc"]
    NKB, NVB = Kg.shape[1], Vg.shape[1]
    bO = bank()
    for q in range(4):
        for h in range(8):
            col = q * 8 + h
            js = [col * 6 + i for i in range(6)]
            for j in js:
                kb = j % NKB
                P.op("pool", lambda e, j=j, kb=kb, h=h: e.indirect_dma_start(
                    out=Kg[:, kb, :], out_offset=None, in_=H["cache_k_rows"],
                    in_offset=bass.IndirectOffsetOnAxis(ap=ridx[:, j:j + 1], axis=0), element_offset=h * 128),
                    r=["ridx"], w=[("Kg", kb)], dma=("Kg", kb))
                vb = j % NVB
                P.op("pool", lambda e, j=j, vb=vb, h=h: e.indirect_dma_start(
                    out=Vg[:, vb, :], out_offset=None, in_=H["cache_v_rows"],
                    in_offset=bass.IndirectOffsetOnAxis(ap=ridx[:, j:j + 1], axis=0), element_offset=h * 128),
                    r=["ridx"], w=[("Vg", vb)], dma=("Vg", vb))
                TT("dve", tmpk[:, :], Kg[:, kb, :], qbc[:, col * 128:(col + 1) * 128], ALU.mult, [("Kg", kb), "qbc"], ["tmpk"])
                P.op("dve", lambda e, j=j: e.tensor_reduce(out=S[:, j:j + 1], in_=tmpk[:, :], axis=AX.X, op=ALU.add),
                     ["tmpk"], [("S", col)])
            P.op("act", lambda e, col=col: e.activation(out=Pm[:, col * 6:(col + 1) * 6], in_=S[:, col * 6:(col + 1) * 6],
                                                       func=AF.Exp, scale=SCALE), [("S", col)], [("Pm", col)])
            for i, j in enumerate(js):
                vb = j % NVB
                MM(psA[:, bO, col:col + 1], Vg[:, vb, :], Pm[:, j:j + 1], i == 0, False, [("Vg", vb), ("Pm", col)], [("ps", bO)])
            MM(psA[:, bO, col:col + 1], Vs[0:4, h, 0:128], Pownb[0:4, col:col + 1], False, True, ["Vs", "Pownb"], [("ps", bO)])
    if H.get('stop', 99) <= 5:
        return
    Pmres = [("Pm", c) for c in range(32)]
    P.op("dve", lambda e: e.tensor_reduce(out=Pr[:, :], in_=Pm[:, :].rearrange("p (c s) -> p c s", s=6), axis=AX.X, op=ALU.add),
         Pmres, ["Pr"])
    bD = bank()
    MM(psA[:, bD, 0:32], H["ones_f"][:, :], Pr[:, :], True, False, ["Pr", "ones_f"], [("ps", bD)])
    MM(psA[:, bD, 0:32], H["ones_f"][0:4, :], Pown[0:4, :], False, True, ["Pown", "ones_f"], [("ps", bD)])
    rden = H["rden"]
    P.op("dve", lambda e: e.reciprocal(out=rden[:, :], in_=psA[:, bD, 0:32]), [("ps", bD)], ["rden", ("ps", bD)])
    TT("dve", H["OTs"][:, :], psA[:, bO, 0:32], rden[:, :], ALU.mult, [("ps", bO), "rden"], ["OTs", ("ps", bO)])
def build(with_sample=True):
    import contextlib
    nc = bass.Bass("TRN2", target_bir_lowering=False)
    P = Prog(nc)
    st = contextlib.ExitStack()

    def din(name, shape, dt=F32):
        return nc.dram_tensor(name, list(shape), dt, kind="ExternalInput").ap()

    def dout(name, shape, dt=F32):
        return nc.dram_tensor(name, list(shape), dt, kind="ExternalOutput").ap()

    def sb(name, shape, dt):
        return st.enter_context(nc.sbuf_tensor(name, list(shape), dt))

    x_own = din("x_own", [NT, D])
    x_prev = din("x_prev", [NTP, D])
    p_own = din("p_own", [NT, 256])
    WSTREAM_ELEMS = (D * 5120 + D * 4096 + 2 * 1024 * D + D * D + D * 2 * FFN + FFN * D + D * D + 256 * D)
    wstream = din("wstream", [WSTREAM_ELEMS])
    w_in, w_gate, w_a_out, w_b_out, w_o = "w_in", "w_gate", "w_a_out", "w_b_out", "w_o"
    w_ffn_in, w_ffn_out, w_ple_gate, w_ple = "w_ffn_in", "w_ffn_out", "w_ple_gate", "w_ple"
    wreg = {}
    wlayout = []
    woff = {"n": 0}
    w_spatial = din("w_spatial", [8, 128, 128])
    b_spatial = din("b_spatial", [1, 1024])
    gvecs = {n: din(n, [1, D]) for n in ("g_pre_mix", "g_post_mix", "g_pre_ffn", "g_post_ffn", "g_ple")}
    g_vnorm = din("g_vnorm", [1, 1024])
    ropeC_own = din("ropeC_own", [32, NT])
    ropeS_own = din("ropeS_own", [32, NT])
    ropeC_prev = din("ropeC_prev", [32, NTP])
    ropeS_prev = din("ropeS_prev", [32, NTP])
    c_ident_bf = din("c_ident_bf", [128, 128], BF16)
    c_ident_f = din("c_ident_f", [128, 128])
    c_perm = din("c_perm", [32, 32], BF16)
    c_tri_bf = din("c_tri_bf", [128, 128], BF16)
    c_tri_f = din("c_tri_f", [128, 128])
    c_negm = din("c_negm", [1, 4 * 64])
    cache_k = din("cache_k", [1280 * 128, 1024])
    cache_v = din("cache_v", [1280 * 128, 1024])
    ptp_d = din("ptp", [64, 2], I32)
    pt128_d = din("pt128", [128, 1], U32)
    c_pair = din("c_pair", [128, 64])
    c_iota = din("c_iota", [128, 192])
    c_cm = din("c_cm", [4, 32])
    c_ones = din("c_ones", [128, 128])
    s_idx = nc.dram_tensor("s_idx", [4, 24], U32).ap()
    s_pg = nc.dram_tensor("s_pg", [96, 2], F32).ap()
    s_q = nc.dram_tensor("s_q", [4, 1024], F32).ap()

    o_k = dout("o_k", [NT, 1024])
    o_v = dout("o_v", [NT, 1024])
    o_gv = dout("o_gv", [4, 1024])
    o_y = dout("o_y", [NT, D])

    outs = []

    ident_bf = sb("ident_bf", [128, 128], BF16)
    ident_f = sb("ident_f", [128, 128], F32)
    perm = sb("perm", [32, 32], BF16)
    tri_bf = sb("tri_bf", [128, 128], BF16)
    tri_f = sb("tri_f", [128, 128], F32)
    negm = sb("negm", [128, 4 * 64], F32)
    NS = 3
    wbuf = sb("wbuf", [128, NS, NKC, WB], BF16)
    gbc = sb("gbc", [128, D], F32)
    kT = sb("kT", [128, NH, 2048 + 4], BF16)
    Vaug = sb("Vaug", [128, 16, NH, 130], BF16)
    Vs = sb("Vs", [4, NH, 130], BF16)
    stat = sb("stat", [128, 256], F32)
    hbt = sb("hbt", [128, D], BF16)
    junk = hbt
    WT = sb("WT", [128, 8, 128], BF16)
    bsb = sb("bsb", [128, 1024], F32)
    kmsum = sb("kmsum", [128, 64], F32)
    kmf = sb("kmf", [128, 64], F32)
    kmhi = sb("kmhi", [128, 64], BF16)
    kmlo = sb("kmlo", [128, 64], BF16)
    kmt = sb("kmt", [128, 64], F32)

    qsT = sb("qsT_sb", [128, NH, 4], BF16)
    OTs = sb("OTs_sb", [128, 32], BF16)
    pt128 = sb("pt128_sb", [128, 1], U32)
    ARENA = 96000
    arena = sb("arena", [128, ARENA // 4], F32)

    def carve_any(name, off, shape, dt):
        return carve(name, off, shape, dt)

    def carve(name, off, shape, dt):
        esz = 2 if dt == BF16 else 4
        n = 1
        for d_ in shape[1:]:
            n *= d_
        nbytes = n * esz
        assert off % 4 == 0 and off + nbytes <= ARENA, (name, off, nbytes)
        P.region(name, off, off + nbytes)
        v = arena[0:shape[0], off // 4: (off + nbytes + 3) // 4]
        if dt != F32:
            v = v.bitcast(dt)[:, 0:n]
        if len(shape) == 3:
            v = v.rearrange("p (a b) -> p a b", a=shape[1])
        elif len(shape) == 4:
            v = v.rearrange("p (a b c) -> p a b c", a=shape[1], b=shape[2])
        return v

    xt = carve("xt", 0, [128, 2, D], F32)
    hT1 = carve("hT1", 16384, [128, NKC, 516], BF16)
    ropeC1 = carve("ropeC1", 32896, [32, 516], F32)
    ropeS1 = carve("ropeS1", 34960, [32, 516], F32)
    kst = carve("kst", 37024, [128, 516], F32)
    kb32 = carve("kb32", 39088, [32, 516], BF16)
    t1 = carve("t1", 40120, [32, 516], F32)
    t2 = carve("t2", 42184, [32, 516], F32)
    ktok = carve("ktok", 44248, [128, 4, 128], F32)
    vtok = carve("vtok", 46296, [128, 2, WB], F32)
    wsp = carve("wsp", 48344, [128, 8, 128], F32)

    H = {}
    H["gbuf"] = [carve("gbuf0", 52440, [128, 4096], F32), carve("gbuf1", 68824, [128, 4096], F32)]
    P.region("gbuf", 52440, 85208)
    H["acc2"] = carve("acc2", 85208, [128, 2048], F32)
    qtok = carve("qtok", 0, [4, 1024], F32)
    H["qbc"] = carve("qbc", 4096, [128, 4096], F32)
    o_ = 52440
    for n_, shp_, dt_ in [("kms_sb", [64, 1024], F32), ("kmsf", [128, 8, 64], F32), ("kmst", [128, 8, 64], F32),
                          ("kmshi", [128, 8, 64], BF16), ("kmslo", [128, 8, 64], BF16), ("gs", [4, 512], F32),
                          ("top8s", [4, 64], F32), ("idx8", [4, 64], U32), ("idx3c", [4, 24], U32), ("nb", [128, 1], U32),
                          ("pg", [128, 2], I32), ("pgf", [128, 2], F32), ("rb96", [128, 2], F32), ("rbb", [128, 192], F32),
                          ("ridx", [128, 192], U32), ("Pown", [4, 32], F32), ("Pownb", [4, 32], BF16), ("Kg", [128, 8, 128], F32),
                          ("Vg", [128, 12, 128], F32), ("tmpk", [128, 128], F32), ("S", [128, 192], F32), ("Pm", [128, 192], F32),
                          ("Pr", [128, 32], F32), ("rden", [128, 32], F32), ("pair", [128, 64], F32), ("iota", [128, 192], F32),
                          ("cm", [4, 32], F32), ("ones_f", [128, 128], F32)]:
        esz_ = 2 if dt_ == BF16 else 4
        nb_ = esz_
        for d_ in shp_[1:]:
            nb_ *= d_
        nb_ = (nb_ + 3) // 4 * 4
        H[n_] = carve_any(n_, o_, shp_, dt_)
        o_ += nb_
    assert o_ <= 85208, o_
    H.update(cache_k_pages=cache_k.rearrange("(pg pos) c -> pg (pos c)", pos=128), cache_k_rows=cache_k, cache_v_rows=cache_v,
             ptp=ptp_d, s_idx=s_idx, s_pg=s_pg, pt128=pt128, ident_f=ident_f, qsT=qsT, kT=kT, Vs=Vs, OTs=OTs)

    NG = 260
    xres = carve("xres", 0, [128, 3, D], F32)
    hT = carve("hT", 24576, [128, NKC, NG], BF16)
    XB = 32896
    mix = carve("mix", XB, [128, 3, D], F32)
    qT = carve("qT", XB, [128, NH, NG], BF16)
    uT = carve("uT", XB + 4160, [128, NH, NG], BF16)
    gaT = carve("gaT", XB + 8320, [128, NKC, NG], BF16)
    YB = 57472
    actT = carve("actT", YB, [128, 44, NG], BF16)
    sa = carve("sa", YB + 22880, [128, 2, NG], F32)
    mixgT = carve("mixgT", YB, [128, NKC, NG], BF16)
    gf = carve("gf", YB + 8320, [128, 3, 1024], F32)
    PT = carve("PT", YB + 8320, [128, 16, 256], BF16)
    gbT = carve("gbT", YB + 8320, [128, NKC, NG], BF16)
    ZB = 82432
    vn = carve("vn", ZB, [128, 3, 1024], BF16)
    ropeC2 = carve("ropeC2", ZB + 6144, [32, NG], F32)
    ropeS2 = carve("ropeS2", ZB + 7184, [32, NG], F32)
    qst = carve("qst", ZB + 8224, [128, NG], F32)
    qb32 = carve("qb32", ZB + 9264, [32, NG], BF16)
    q1 = carve("q1", ZB + 9784, [32, NG], F32)
    q2 = carve("q2", ZB + 10824, [32, NG], F32)
    acc = carve("acc", ZB + 6144, [128, 2, 130], F32)
    Otok = carve("Otok", ZB + 7184, [128, 2, 1024], BF16)
    gm = carve("gm", ZB + 11280, [128, 2, 64], F32)
    top8 = carve("top8", ZB + 11792, [128, 2, 64], F32)
    thr = carve("thr", ZB + 12304, [128, 2, 8], F32)
    sel = carve("sel", ZB + 12368, [128, 2, 64], F32)
    rsum = carve("rsum", ZB + 12880, [128, 4], F32)
    ta = carve("ta", ZB, [128, 2, NG], F32)
    tb = carve("tb", ZB + 2080, [128, 2, NG], F32)
    pst = carve("pst", ZB, [128, 256], F32)
    pbt = carve("pbt", ZB + 1024, [128, 256], BF16)
    pT = carve("pT", ZB + 1536, [128, 2, NG], BF16)
    tg = carve("tg", ZB + 2576, [128, WB], F32)

    psA = st.enter_context(nc.psum_tensor("psA", [128, 6, 512], F32))
    psT = st.enter_context(nc.psum_tensor("psT", [128, 2, 1024], BF16))

    cnt = {"ps": 0, "stat": 0, "xt": 0, "ktok": 0, "vtok": 0, "acc": 0}

    def bank():
        b = cnt["ps"] % 6
        cnt["ps"] += 1
        return b

    def statcol():
        sc = (cnt["stat"] % 32) * 8
        cnt["stat"] += 1
        return sc

    def dma(eng, out, in_, r, w, key):
        return P.op(eng, lambda e: e.dma_start(out=out, in_=in_), r=r, w=w, dma=key)

    def ACT(out, in_, func, r, w, **kw):
        return P.op("act", lambda e: e.activation(out=out, in_=in_, func=func, **kw), r, w)

    def TT(eng, out, in0, in1, op, r, w):
        return P.op(eng, lambda e: e.tensor_tensor(out=out, in0=in0, in1=in1, op=op), r, w)

    def TS(eng, out, in0, s1, s2, op0, op1, r, w):
        if s2 is None:
            return P.op(eng, lambda e: e.tensor_scalar(out=out, in0=in0, scalar1=s1, scalar2=None, op0=op0), r, w)
        return P.op(eng, lambda e: e.tensor_scalar(out=out, in0=in0, scalar1=s1, scalar2=s2, op0=op0, op1=op1), r, w)

    def STT(eng, out, in0, scalar, in1, op0, op1, r, w):
        return P.op(eng, lambda e: e.scalar_tensor_tensor(out=out, in0=in0, scalar=scalar, in1=in1, op0=op0, op1=op1), r, w)

    def COPY(eng, out, in_, r, w):
        if eng == "act":
            return ACT(out, in_, AF.Copy, r, w)
        return P.op(eng, lambda e: e.tensor_copy(out=out, in_=in_), r, w)

    def MM(out, lhsT, rhs, start, stop, r, w):
        return P.op("pe", lambda e: e.matmul(out, lhsT=lhsT, rhs=rhs, start=start, stop=stop), r, w)

    def TR(out, in_, ident, r, w):
        return P.op("pe", lambda e: e.transpose(out, in_, ident), r, w)

    def RECIP(out, in_, r, w):
        return P.op("dve", lambda e: e.reciprocal(out=out, in_=in_), r, w)

    wjobs = []

    def wblock(w, r0, nkc, c0):
        key = (w, r0, nkc, c0)
        if key not in wreg:
            wreg[key] = woff["n"]
            wlayout.append(key)
            woff["n"] += 128 * nkc * WB
        off = wreg[key]
        wjobs.append((wstream[off:off + 128 * nkc * WB].rearrange("(p f) -> p f", p=128), nkc))

    for g1 in range(4):
        for blk in range(4):
            wblock(w_in, 0, 16, 1024 + blk * WB)
        for blk in range(4):
            wblock(w_in, 0, 16, 2048 + blk * WB)
    for blk in range(4):
        wblock(w_in, 0, 16, blk * WB)
    for gi in range(4):
        for blk in range(4):
            wblock(w_in, 0, 16, blk * WB)
        for blk in range(4):
            wblock(w_in, 0, 16, 3072 + blk * WB)
        for blk in range(4):
            wblock(w_in, 0, 16, 4096 + blk * WB)
        for blk in range(16):
            wblock(w_gate, 0, 16, blk * WB)
        for blk in range(8):
            wblock(w_a_out, 0, 8, blk * WB)
            wblock(w_b_out, 0, 8, blk * WB)
        for blk in range(8):
            wblock(w_o, 0, 16, blk * WB)
        for j in range(22):
            wblock(w_ffn_in, 0, 16, j * WB)
            wblock(w_ffn_in, 0, 16, FFN + j * WB)
        for cb in range(8):
            wblock(w_ffn_out, 0, 16, cb * WB)
            wblock(w_ffn_out, 2048, 16, cb * WB)
            wblock(w_ffn_out, 4096, 12, cb * WB)
        for blk in range(8):
            wblock(w_ple_gate, 0, 16, blk * WB)
        for blk in range(8):
            wblock(w_ple, 0, 2, blk * WB)
    wstate = {"issued": 0, "used": 0}

    def w_next(expect_nkc):
        while wstate["issued"] < min(len(wjobs), wstate["used"] + NS):
            j = wstate["issued"]
            ap, nkc = wjobs[j]
            s = j % NS
            dma("pool", wbuf[:, s, 0:nkc, :].rearrange("p k c -> p (k c)"), ap, [], [("w", s)], ("w", s))
            wstate["issued"] += 1
        j = wstate["used"]
        assert wjobs[j][1] == expect_nkc, (j, wjobs[j][1], expect_nkc)
        wstate["used"] += 1
        return j % NS

    dma("sp", ident_bf[:], c_ident_bf, [], ["ident_bf"], "c0")
    dma("sp", ident_f[:], c_ident_f, [], ["ident_f"], "c1")
    dma("sp", perm[:], c_perm, [], ["perm"], "c2")
    dma("sp", tri_bf[:], c_tri_bf, [], ["tri_bf"], "c3")
    dma("sp", tri_f[:], c_tri_f, [], ["tri_f"], "c4")
    dma("sp", negm[:], c_negm.partition_broadcast(128), [], ["negm"], "c5")
    dma("sp", bsb[:], b_spatial.partition_broadcast(128), [], ["bsb"], "c6")
    P.op("pool", lambda e: e.memset(Vaug[:, :, :, 128:130], 1.0), [], [("Vaug", k) for k in range(16)])
    P.op("pool", lambda e: e.memset(Vs[:, :, 128:130], 1.0), [], ["Vs"])
    P.op("pool", lambda e: e.memset(kmsum[:], 0.0), [], ["kmsum"])

    def load_g(ap, width=D):
        dma("sp", gbc[:, 0:width], ap.partition_broadcast(128), [], ["gbc"], "gbc")

    dma("sp", wsp[:], w_spatial.rearrange("g t s -> t g s"), [], ["wsp"], "c7")
    for g in range(8):
        b = bank()
        TR(psA[:, b, 0:128], wsp[:, g, :], ident_f[:, :], ["wsp", "ident_f"], [("ps", b)])
        TT("dve", WT[:, g, :], psA[:, b, 0:128], tri_f[:, :], ALU.mult, [("ps", b), "tri_f"], [("WT", g), ("ps", b)])
    WTres = [("WT", g) for g in range(8)]

    def rstd_of(src, M, src_res, width):
        sc = statcol()
        sres = ("stat", sc)
        ACT(junk[0:M, 0:width], src, AF.Square, src_res, ["hbt", sres], accum_out=stat[0:M, sc:sc + 1])
        TS("dve", stat[0:M, sc + 1:sc + 2], stat[0:M, sc:sc + 1], 1.0 / width, EPS, ALU.mult, ALU.add, [sres], [sres])
        ACT(stat[0:M, sc + 2:sc + 3], stat[0:M, sc + 1:sc + 2], AF.Sqrt, [sres], [sres])
        RECIP(stat[0:M, sc + 3:sc + 4], stat[0:M, sc + 2:sc + 3], [sres], [sres])
        return stat[0:M, sc + 3:sc + 4], sres

    def norm_to_T(src, M, src_res, dstT, tok0, dst_res):
        rs, sres = rstd_of(src, M, src_res, D)
        STT("dve", hbt[0:M, :], src, rs, gbc[0:M, :], ALU.mult, ALU.mult, src_res + [sres, "gbc"], ["hbt"])
        for b in range(2):
            for k in range(8):
                kc = b * 8 + k
                TR(psT[:, b, k * 128:k * 128 + M], hbt[0:M, kc * 128:(kc + 1) * 128], ident_bf[0:M, 0:M],
                   ["hbt", "ident_bf"], [("psT", b)])
            src_ps = psT[:, b, :].rearrange("p (k t) -> p k t", k=8)[:, :, 0:M]
            COPY("act" if b == 0 else "dve", dstT[:, b * 8:(b + 1) * 8, tok0:tok0 + M], src_ps, [("psT", b)], [dst_res, ("psT", b)])

    def rope_evac(b, n, stg, stg_res, b32, b32_res, ta_, ta_res, tb_, tb_res, rC, rS, c0):
        ACT(stg[:, c0:c0 + n], psA[:, b, 0:n], AF.Copy, [("ps", b)], [stg_res, ("ps", b)])
        COPY("dve", b32[:, c0:c0 + n], stg[0:32, c0:c0 + n], [stg_res], [b32_res])
        b2 = bank()
        MM(psA[0:32, b2, 0:n], perm[:, :], b32[:, c0:c0 + n], True, True, [b32_res, "perm"], [("ps", b2)])
        TT("dve", ta_[:, c0:c0 + n], stg[0:32, c0:c0 + n], rC[:, c0:c0 + n], ALU.mult, [stg_res, "rope"], [ta_res])
        TT("dve", tb_[:, c0:c0 + n], psA[0:32, b2, 0:n], rS[:, c0:c0 + n], ALU.mult, [("ps", b2), "rope"], [tb_res, ("ps", b2)])
        TT("dve", stg[0:32, c0:c0 + n], ta_[:, c0:c0 + n], tb_[:, c0:c0 + n], ALU.add, [ta_res, tb_res], [stg_res])

    H["psA"] = psA
    dma("sp", pt128[:], pt128_d, [], ["pt128"], "c8")
    scan = {"c": 0}

    def scan_step():
        if scan["c"] < 32:
            emit_sample_scan_chunk(P, nc, scan["c"], H)
            scan["c"] += 1

    load_g(gvecs["g_pre_mix"])
    for g1 in range(4):
        prev = g1 < 2
        base_tok = (g1 % 2) * 512
        last = g1 == 3
        kbase = g1 * 512
        tiles = [(t * 128, 128) for t in range(4)] + ([(512, 4)] if last else [])
        for ti, (tk, M) in enumerate(tiles):
            if M == 128:
                src = (x_prev if prev else x_own)[base_tok + tk: base_tok + tk + 128, :]
            else:
                src = x_own[1024:1028, :]
            xs = cnt["xt"] % 2
            cnt["xt"] += 1
            dma("sp", xt[0:M, xs, :], src, [], [("xt", xs)], ("xt", xs))
            norm_to_T(xt[0:M, xs, :], M, [("xt", xs)], hT1, tk, ("hT1", ti))
        hres = [("hT1", t) for t in range(len(tiles))]
        rc = (ropeC_prev if prev else ropeC_own)
        rs_ = (ropeS_prev if prev else ropeS_own)
        dma("sp", ropeC1[:, 0:512], rc[:, base_tok:base_tok + 512], [], ["rope"], "rc")
        dma("sp", ropeS1[:, 0:512], rs_[:, base_tok:base_tok + 512], [], ["rope"], "rc")
        if last:
            dma("sp", ropeC1[:, 512:516], ropeC_own[:, 1024:1028], [], ["rope"], "rc")
            dma("sp", ropeS1[:, 512:516], ropeS_own[:, 1024:1028], [], ["rope"], "rc")
        chunks = [(0, 512)] + ([(512, 4)] if last else [])
        for blk in range(4):
            s = w_next(16)
            for j in range(2):
                h = blk * 2 + j
                for (c0, n) in chunks:
                    b = bank()
                    for kc in range(NKC):
                        MM(psA[:, b, 0:n], wbuf[:, s, kc, j * 128:(j + 1) * 128], hT1[:, kc, c0:c0 + n], kc == 0, kc == NKC - 1,
                           [("w", s)] + hres, [("ps", b)])
                    rope_evac(b, n, kst, "kst", kb32, "kb32", t1, "t1", t2, "t2", ropeC1, ropeS1, c0)
                    kofs = kbase + c0 if c0 == 0 else 2048
                    ACT(kT[:, h, kofs:kofs + n], kst[:, c0:c0 + n], AF.Copy, ["kst"], [("kT", h, g1, c0)])
                    if c0 == 0:
                        P.op("dve", lambda e, h=h, g1=g1: e.tensor_reduce(
                            out=kmsum[:, h * 8 + g1 * 2: h * 8 + g1 * 2 + 2],
                            in_=kst[:, 0:512].rearrange("p (b k) -> p b k", b=2), axis=AX.X, op=ALU.add),
                             ["kst"], [("kmsum", h, g1)])
                    if not prev:
                        ttiles = [(c0 + i * 128, 128) for i in range(4)] if n == 512 else [(c0, 4)]
                        for (tk, M) in ttiles:
                            b3 = bank()
                            TR(psA[0:M, b3, 0:128], kst[:, tk:tk + M], ident_f[:, :], ["kst", "ident_f"], [("ps", b3)])
                            ks = cnt["ktok"] % 4
                            cnt["ktok"] += 1
                            COPY("dve", ktok[0:M, ks, :], psA[0:M, b3, 0:128], [("ps", b3)], [("ktok", ks), ("ps", b3)])
                            row0 = (base_tok + tk) if n == 512 else 1024
                            outs.append(dma("sp", o_k[row0:row0 + M, h * 128:(h + 1) * 128], ktok[0:M, ks, :],
                                            [("ktok", ks)], [], ("ktok", ks)))
            scan_step()
        for blk in range(4):
            s = w_next(16)
            h0 = blk * 2
            for (tk, M) in tiles:
                b = bank()
                for kc in range(NKC):
                    MM(psA[0:M, b, 0:WB], hT1[:, kc, tk:tk + M], wbuf[:, s, kc, :], kc == 0, kc == NKC - 1,
                       [("w", s)] + hres, [("ps", b)])
                src = psA[0:M, b, 0:WB].rearrange("p (h d) -> p h d", h=2)
                if M == 128:
                    kt = (kbase + tk) // 128
                    ACT(Vaug[:, kt, h0:h0 + 2, 0:128], src, AF.Copy, [("ps", b)], [("Vaug", kt), ("ps", b)])
                else:
                    ACT(Vs[:, h0:h0 + 2, 0:128], src, AF.Copy, [("ps", b)], ["Vs", ("ps", b)])
                if not prev:
                    vs_ = cnt["vtok"] % 2
                    cnt["vtok"] += 1
                    COPY("dve", vtok[0:M, vs_, :], psA[0:M, b, 0:WB], [("ps", b)], [("vtok", vs_), ("ps", b)])
                    row0 = (base_tok + tk) if M == 128 else 1024
                    outs.append(dma("sp", o_v[row0:row0 + M, h0 * 128:h0 * 128 + WB], vtok[0:M, vs_, :],
                                    [("vtok", vs_)], [], ("vtok", vs_)))
            scan_step()

    for blk in range(4):
        s = w_next(16)
        for j in range(2):
            h = blk * 2 + j
            b = bank()
            for kc in range(NKC):
                MM(psA[:, b, 0:4], wbuf[:, s, kc, j * 128:(j + 1) * 128], hT1[:, kc, 512:516], kc == 0, kc == NKC - 1,
                   [("w", s)] + hres, [("ps", b)])
            rope_evac(b, 4, kst, "kst", kb32, "kb32", t1, "t1", t2, "t2", ropeC1, ropeS1, 512)
            ACT(qsT[:, h, 0:4], kst[:, 512:516], AF.Copy, ["kst"], ["qsT"])
            b3 = bank()
            TR(psA[0:4, b3, 0:128], kst[:, 512:516], ident_f[:, :], ["kst", "ident_f"], [("ps", b3)])
            COPY("dve", qtok[0:4, h * 128:(h + 1) * 128], psA[0:4, b3, 0:128], [("ps", b3)], ["qtok", ("ps", b3)])
    dma("sp", s_q, qtok[0:4, :], ["qtok"], ["s_q"], "s_q")
    dma("sp", H["qbc"][:, :], s_q.rearrange("a (b o) -> o (a b)", o=1).partition_broadcast(128), ["s_q"], ["qbc"], "qbc")
    while scan["c"] < 32:
        scan_step()
    dma("sp", H["pair"][:, :], c_pair, [], ["pair"], "c9")
    dma("sp", H["iota"][:, :], c_iota, [], ["iota"], "c10")
    dma("sp", H["cm"][:, :], c_cm, [], ["cm"], "c11")
    dma("sp", H["ones_f"][:, :], c_ones, [], ["ones_f"], "c12")
    emit_sample_attention(P, nc, H, bank)

    kmres = [("kmsum", h, g1) for h in range(8) for g1 in range(4)] + ["kmsum"]
    TS("dve", kmf[:], kmsum[:], 1.0 / 256, None, ALU.mult, None, kmres, ["kmf"])
    COPY("dve", kmhi[:], kmf[:], ["kmf"], ["kmhi"])
    COPY("dve", kmt[:], kmhi[:], ["kmhi"], ["kmt"])
    TT("dve", kmt[:], kmf[:], kmt[:], ALU.subtract, ["kmf", "kmt"], ["kmt"])
    COPY("dve", kmlo[:], kmt[:], ["kmt"], ["kmlo"])

    SCALE = float(HD) ** -0.5
    for gi in range(4):
        last = gi == 3
        ntok = NG if last else 256
        tok0 = gi * 256
        tiles = [(0, 0, 128), (1, 128, 128)] + ([(2, 256, 4)] if last else [])
        n_own = 4 + gi
        nkt = 2 * n_own + 2

        load_g(gvecs["g_pre_mix"])
        for (ti, tk, M) in tiles:
            r0 = tok0 + tk if M == 128 else 1024
            dma("sp", xres[0:M, ti, :], x_own[r0:r0 + M, :], [], [("xres", ti)], ("xres", ti))
            norm_to_T(xres[0:M, ti, :], M, [("xres", ti)], hT, tk, ("hT", ti))
        hres = [("hT", ti) for (ti, _, _) in tiles]
        dma("sp", ropeC2[:, 0:256], ropeC_own[:, tok0:tok0 + 256], [], ["rope"], "rc")
        dma("sp", ropeS2[:, 0:256], ropeS_own[:, tok0:tok0 + 256], [], ["rope"], "rc")
        if last:
            dma("sp", ropeC2[:, 256:260], ropeC_own[:, 1024:1028], [], ["rope"], "rc")
            dma("sp", ropeS2[:, 256:260], ropeS_own[:, 1024:1028], [], ["rope"], "rc")

        for blk in range(4):
            s = w_next(16)
            for j in range(2):
                h = blk * 2 + j
                b = bank()
                for kc in range(NKC):
                    MM(psA[:, b, 0:ntok], wbuf[:, s, kc, j * 128:(j + 1) * 128], hT[:, kc, 0:ntok], kc == 0, kc == NKC - 1,
                       [("w", s)] + hres, [("ps", b)])
                rope_evac(b, ntok, qst, "qst", qb32, "qb32", q1, "q1", q2, "q2", ropeC2, ropeS2, 0)
                ACT(qT[:, h, 0:ntok], qst[:, 0:ntok], AF.Copy, ["qst"], [("qT", h)])
        for blk in range(4):
            s = w_next(16)
            for j in range(2):
                c = blk * 2 + j
                b = bank()
                for kc in range(NKC):
                    MM(psA[:, b, 0:ntok], wbuf[:, s, kc, j * 128:(j + 1) * 128], hT[:, kc, 0:ntok], kc == 0, kc == NKC - 1,
                       [("w", s)] + hres, [("ps", b)])
                ACT(uT[:, c, 0:ntok], psA[:, b, 0:ntok], AF.Gelu_apprx_tanh, [("ps", b)], [("uT", c), ("ps", b)])
        for blk in range(4):
            s = w_next(16)
            for (ti, tk, M) in tiles:
                b = bank()
                for kc in range(NKC):
                    MM(psA[0:M, b, 0:WB], hT[:, kc, tk:tk + M], wbuf[:, s, kc, :], kc == 0, kc == NKC - 1,
                       [("w", s)] + hres, [("ps", b)])
                ACT(gf[0:M, ti, blk * WB:(blk + 1) * WB], psA[0:M, b, 0:WB], AF.Gelu_apprx_tanh, [("ps", b)],
                    [("gf", ti, blk), ("ps", b)])
        load_g(g_vnorm, 1024)
        for (ti, tk, M) in tiles:
            gres = [("gf", ti, blk) for blk in range(4)]
            sc = statcol()
            sres = ("stat", sc)
            ACT(junk[0:M, 0:1024], gf[0:M, ti, :], AF.Copy, gres, ["hbt", sres], accum_out=stat[0:M, sc:sc + 1])
            ACT(junk[0:M, 0:1024], gf[0:M, ti, :], AF.Square, gres, ["hbt", sres], accum_out=stat[0:M, sc + 1:sc + 2])
            TS("dve", stat[0:M, sc + 2:sc + 3], stat[0:M, sc:sc + 1], 1.0 / 1024, None, ALU.mult, None, [sres], [sres])
            TT("dve", stat[0:M, sc + 3:sc + 4], stat[0:M, sc + 2:sc + 3], stat[0:M, sc + 2:sc + 3], ALU.mult, [sres], [sres])
            TS("dve", stat[0:M, sc + 4:sc + 5], stat[0:M, sc + 1:sc + 2], 1.0 / 1024, EPS, ALU.mult, ALU.add, [sres], [sres])
            TT("dve", stat[0:M, sc + 4:sc + 5], stat[0:M, sc + 4:sc + 5], stat[0:M, sc + 3:sc + 4], ALU.subtract, [sres], [sres])
            ACT(stat[0:M, sc + 5:sc + 6], stat[0:M, sc + 4:sc + 5], AF.Sqrt, [sres], [sres])
            RECIP(stat[0:M, sc + 6:sc + 7], stat[0:M, sc + 5:sc + 6], [sres], [sres])
            TS("dve", gf[0:M, ti, :], gf[0:M, ti, :], stat[0:M, sc + 2:sc + 3], stat[0:M, sc + 6:sc + 7], ALU.subtract, ALU.mult,
               gres + [sres], gres)
            if M == 4:
                TT("dve", gf[0:M, ti, :], gf[0:M, ti, :], gbc[0:M, 0:1024], ALU.mult, gres + ["gbc"], gres)
                outs.append(dma("sp", o_gv[:, :], gf[0:M, ti, :], gres, [], "ogv"))
                COPY("dve", vn[0:M, ti, :], gf[0:M, ti, :], gres, [("vn", ti)])
            else:
                TT("dve", vn[0:M, ti, :], gf[0:M, ti, :], gbc[0:M, 0:1024], ALU.mult, gres + ["gbc"], [("vn", ti)])
        for (ti, tk, M) in tiles:
            for g in range(8):
                b = bank()
                MM(psA[:, b, 0:M], vn[0:M, ti, g * 128:(g + 1) * 128], WT[0:M, g, 0:M], True, True,
                   [("vn", ti), ("WT", g)], [("ps", b)])
                TT("dve", psA[:, b, 0:M], psA[:, b, 0:M], bsb[:, g * 128:g * 128 + M], ALU.add, [("ps", b), "bsb"], [("ps", b)])
                TT("dve", uT[:, g, tk:tk + M], psA[:, b, 0:M], uT[:, g, tk:tk + M], ALU.mult, [("ps", b), ("uT", g)],
                   [("uT", g), ("ps", b)])
        BTres = [("uT", g) for g in range(8)]

        for qt in range(2):
            b = bank()
            for h in range(8):
                MM(psA[:, b, h * 8:(h + 1) * 8], qT[:, h, qt * 128:(qt + 1) * 128], kmhi[:, h * 8:(h + 1) * 8], True, False,
                   [("qT", h), "kmhi"], [("ps", b)])
                MM(psA[:, b, h * 8:(h + 1) * 8], qT[:, h, qt * 128:(qt + 1) * 128], kmlo[:, h * 8:(h + 1) * 8], False, True,
                   [("qT", h), "kmlo"], [("ps", b)])
            gmr = ("gm", qt)
            TT("dve", gm[:, qt, :], psA[:, b, 0:64], negm[:, gi * 64:(gi + 1) * 64], ALU.add, [("ps", b), "negm"], [gmr, ("ps", b)])
            for h in range(8):
                P.op("dve", lambda e, h=h, qt=qt: e.max(out=top8[:, qt, h * 8:(h + 1) * 8], in_=gm[:, qt, h * 8:(h + 1) * 8]),
                     [gmr], [gmr])
            TS("dve", thr[:, qt, :], top8[:, qt, :].rearrange("p (h k) -> p h k", k=8)[:, :, 2], -1e29, None, ALU.max, None,
               [gmr], [gmr])
            for h in range(8):
                TS("dve", sel[:, qt, h * 8:(h + 1) * 8], gm[:, qt, h * 8:(h + 1) * 8], thr[:, qt, h:h + 1], None, ALU.is_ge, None,
                   [gmr], [gmr])
        ka, kb_ = 2 * n_own, 2 * n_own + 1
        for h in range(8):
            for kt in range(nkt):
                bq = bank()
                MM(psA[:, bq, 0:256], kT[:, h, kt * 128:(kt + 1) * 128], qT[:, h, 0:256], True, True, [("qT", h)], [("ps", bq)])
                ACT(PT[:, kt, :], psA[:, bq, 0:256], AF.Exp, [("ps", bq)], [("PT", kt), ("ps", bq)], scale=SCALE)
            TT("pool", PT[:, ka, 0:128], PT[:, ka, 0:128], tri_bf[:, :], ALU.mult, [("PT", ka), "tri_bf"], [("PT", ka)])
            TT("pool", PT[:, kb_, 128:256], PT[:, kb_, 128:256], tri_bf[:, :], ALU.mult, [("PT", kb_), "tri_bf"], [("PT", kb_)])
            for qt in range(2):
                a = cnt["acc"] % 2
                cnt["acc"] += 1
                ar = ("acc", a)
                gmr = ("gm", qt)
                for n in range(n_own + 1):
                    kts = [2 * n, 2 * n + 1]
                    if n == n_own and qt == 0:
                        kts = [2 * n]
                    b = bank()
                    for i_, kt in enumerate(kts):
                        MM(psA[:, b, 0:129], PT[:, kt, qt * 128:(qt + 1) * 128], Vaug[:, kt, h, 0:129], i_ == 0, i_ == len(kts) - 1,
                           [("PT", kt), ("Vaug", kt)], [("ps", b)])
                    sc_ = sel[:, qt, h * 8 + n:h * 8 + n + 1] if n < n_own else None
                    if n == 0:
                        TS("dve", acc[:, a, 0:129], psA[:, b, 0:129], sc_, None, ALU.mult, None, [("ps", b), gmr], [ar, ("ps", b)])
                    elif n < n_own:
                        STT("dve", acc[:, a, 0:129], psA[:, b, 0:129], sc_, acc[:, a, 0:129], ALU.mult, ALU.add,
                            [("ps", b), gmr, ar], [ar, ("ps", b)])
                    else:
                        TT("dve", acc[:, a, 0:129], psA[:, b, 0:129], acc[:, a, 0:129], ALU.add, [("ps", b), ar], [ar, ("ps", b)])
                RECIP(rsum[:, a:a + 1], acc[:, a, 128:129], [ar], [ar])
                TS("dve", Otok[:, qt, h * 128:(h + 1) * 128], acc[:, a, 0:128], rsum[:, a:a + 1], None, ALU.mult, None, [ar],
                   [("Otok", qt, h)])
        for qt in range(2):
            for h in range(8):
                TR(psT[:, qt, h * 128:(h + 1) * 128], Otok[:, qt, h * 128:(h + 1) * 128], ident_bf[:, :],
                   [("Otok", qt, h), "ident_bf"], [("psT", qt)])
            COPY("act", qT[:, :, qt * 128:(qt + 1) * 128], psT[:, qt, :].rearrange("p (k t) -> p k t", k=8), [("psT", qt)],
                 [("qT", h) for h in range(8)] + [("psT", qt)])
        if last:
            COPY("dve", qT[:, :, 256:260], OTs[:, :].rearrange("p (q h) -> p h q", h=8), ["OTs"], [("qT", h) for h in range(8)])
        OTres = [("qT", h) for h in range(8)]

        for blk in range(16):
            s = w_next(16)
            dst = gaT if blk < 8 else gbT
            dn = "gaT" if blk < 8 else "gbT"
            for j in range(2):
                c = (blk % 8) * 2 + j
                b = bank()
                for kc in range(NKC):
                    MM(psA[:, b, 0:ntok], wbuf[:, s, kc, j * 128:(j + 1) * 128], hT[:, kc, 0:ntok], kc == 0, kc == NKC - 1,
                       [("w", s)] + hres, [("ps", b)])
                ACT(dst[:, c, 0:ntok], psA[:, b, 0:ntok], AF.Sigmoid, [("ps", b)], [(dn, c), ("ps", b)])
        for blk in range(8):
            sa_ = w_next(8)
            for j in range(2):
                c = blk * 2 + j
                b = bank()
                for kc in range(8):
                    MM(psA[:, b, 0:ntok], wbuf[:, sa_, kc, j * 128:(j + 1) * 128], qT[:, kc, 0:ntok], kc == 0, kc == 7,
                       [("w", sa_)] + OTres, [("ps", b)])
                TT("dve", ta[:, j, 0:ntok], psA[:, b, 0:ntok], gaT[:, c, 0:ntok], ALU.mult, [("ps", b), ("gaT", c)], [("ta", j), ("ps", b)])
            sb_ = w_next(8)
            for j in range(2):
                c = blk * 2 + j
                b = bank()
                for kc in range(8):
                    MM(psA[:, b, 0:ntok], wbuf[:, sb_, kc, j * 128:(j + 1) * 128], uT[:, kc, 0:ntok], kc == 0, kc == 7,
                       [("w", sb_)] + BTres, [("ps", b)])
                TT("dve", tb[:, j, 0:ntok], psA[:, b, 0:ntok], gbT[:, c, 0:ntok], ALU.mult, [("ps", b), ("gbT", c)], [("tb", j), ("ps", b)])
                TT("pool", mixgT[:, c, 0:ntok], ta[:, j, 0:ntok], tb[:, j, 0:ntok], ALU.add, [("ta", j), ("tb", j)], [("mixgT", c)])
        mres = [("mixgT", c) for c in range(16)]

        def resid_phase(wname, nblk_w, lhs_buf, lhs_res, gpost, gnext, sub_rows=None):
            pass

        for blk in range(8):
            s = w_next(16)
            for (ti, tk, M) in tiles:
                b = bank()
                for kc in range(NKC):
                    MM(psA[0:M, b, 0:WB], mixgT[:, kc, tk:tk + M], wbuf[:, s, kc, :], kc == 0, kc == NKC - 1,
                       [("w", s)] + mres, [("ps", b)])
                ACT(mix[0:M, ti, blk * WB:(blk + 1) * WB], psA[0:M, b, 0:WB], AF.Copy, [("ps", b)], [("mix", ti, blk), ("ps", b)])

        def post_norm_residual(gname_post, gname_next, dstT):
            load_g(gvecs[gname_post])
            for (ti, tk, M) in tiles:
                mr = [("mix", ti, blk) for blk in range(8)]
                rs, sres = rstd_of(mix[0:M, ti, :], M, mr, D)
                STT("dve", mix[0:M, ti, :], mix[0:M, ti, :], rs, gbc[0:M, :], ALU.mult, ALU.mult, mr + [sres, "gbc"], mr)
                TT("pool", xres[0:M, ti, :], xres[0:M, ti, :], mix[0:M, ti, :], ALU.add, mr + [("xres", ti)], [("xres", ti)])
            load_g(gvecs[gname_next])
            for (ti, tk, M) in tiles:
                norm_to_T(xres[0:M, ti, :], M, [("xres", ti)], dstT, tk, ("hT", ti))

        post_norm_residual("g_post_mix", "g_pre_ffn", hT)

        for j2 in range(22):
            s1 = w_next(16)
            for j in range(2):
                b = bank()
                for kc in range(NKC):
                    MM(psA[:, b, 0:ntok], wbuf[:, s1, kc, j * 128:(j + 1) * 128], hT[:, kc, 0:ntok], kc == 0, kc == NKC - 1,
                       [("w", s1)] + hres, [("ps", b)])
                ACT(sa[:, j, 0:ntok], psA[:, b, 0:ntok], AF.Silu, [("ps", b)], [("sa", j), ("ps", b)])
            s2 = w_next(16)
            for j in range(2):
                c = j2 * 2 + j
                b = bank()
                for kc in range(NKC):
                    MM(psA[:, b, 0:ntok], wbuf[:, s2, kc, j * 128:(j + 1) * 128], hT[:, kc, 0:ntok], kc == 0, kc == NKC - 1,
                       [("w", s2)] + hres, [("ps", b)])
                TT("dve", actT[:, c, 0:ntok], psA[:, b, 0:ntok], sa[:, j, 0:ntok], ALU.mult, [("ps", b), ("sa", j)],
                   [("actT", c), ("ps", b)])
        ares = [("actT", c) for c in range(44)]

        for cb in range(8):
            banks = [bank() for _ in tiles]
            for sub in range(3):
                nk = 16 if sub < 2 else 12
                s = w_next(nk)
                for (ti, tk, M) in tiles:
                    b = banks[ti]
                    for kc in range(nk):
                        MM(psA[0:M, b, 0:WB], actT[:, sub * 16 + kc, tk:tk + M], wbuf[:, s, kc, :], sub == 0 and kc == 0,
                           sub == 2 and kc == nk - 1, [("w", s)] + ares, [("ps", b)])
            for (ti, tk, M) in tiles:
                b = banks[ti]
                ACT(mix[0:M, ti, cb * WB:(cb + 1) * WB], psA[0:M, b, 0:WB], AF.Copy, [("ps", b)], [("mix", ti, cb), ("ps", b)])
        post_norm_residual("g_post_ffn", "g_ple", hT)

        for (ti, tk, M) in tiles:
            r0 = tok0 + tk if M == 128 else 1024
            dma("sp", pst[0:M, :], p_own[r0:r0 + M, :], [], ["pst"], "pst")
            COPY("dve", pbt[0:M, :], pst[0:M, :], ["pst"], ["pbt"])
            for kc in range(2):
                TR(psT[:, 0, kc * 128:kc * 128 + M], pbt[0:M, kc * 128:(kc + 1) * 128], ident_bf[0:M, 0:M], ["pbt", "ident_bf"],
                   [("psT", 0)])
            COPY("act", pT[:, :, tk:tk + M], psT[:, 0, 0:256].rearrange("p (k t) -> p k t", k=2)[:, :, 0:M], [("psT", 0)],
                 [("pT", ti), ("psT", 0)])
        pres = [("pT", ti) for (ti, _, _) in tiles]
        for blk in range(8):
            s = w_next(16)
            for (ti, tk, M) in tiles:
                b = bank()
                for kc in range(NKC):
                    MM(psA[0:M, b, 0:WB], hT[:, kc, tk:tk + M], wbuf[:, s, kc, :], kc == 0, kc == NKC - 1,
                       [("w", s)] + hres, [("ps", b)])
                ACT(mix[0:M, ti, blk * WB:(blk + 1) * WB], psA[0:M, b, 0:WB], AF.Sigmoid, [("ps", b)], [("mix", ti, blk), ("ps", b)])
        for blk in range(8):
            s = w_next(2)
            for (ti, tk, M) in tiles:
                b = bank()
                for kc in range(2):
                    MM(psA[0:M, b, 0:WB], pT[:, kc, tk:tk + M], wbuf[:, s, kc, :], kc == 0, kc == 1,
                       [("w", s)] + pres, [("ps", b)])
                TT("dve", mix[0:M, ti, blk * WB:(blk + 1) * WB], psA[0:M, b, 0:WB], mix[0:M, ti, blk * WB:(blk + 1) * WB], ALU.mult,
                   [("ps", b), ("mix", ti, blk)], [("mix", ti, blk), ("ps", b)])
        for (ti, tk, M) in tiles:
            mr = [("mix", ti, blk) for blk in range(8)]
            TT("pool", xres[0:M, ti, :], xres[0:M, ti, :], mix[0:M, ti, :], ALU.add, mr + [("xres", ti)], [("xres", ti)])
            r0 = tok0 + tk if M == 128 else 1024
            outs.append(dma("sp", o_y[r0:r0 + M, :], xres[0:M, ti, :], [("xres", ti)], [], ("oy", ti)))

    assert wstate["used"] == len(wjobs), (wstate, len(wjobs))
    assert woff["n"] == WSTREAM_ELEMS, (woff["n"], WSTREAM_ELEMS)
    P.emit(outs)
    st.close()
    return nc, wlayout


_CACHE = {}


def kernel(x_prompt, x_sample, cache_k, cache_v, page_table, p_prompt, p_sample,
           g_pre_mix, w_in, g_vnorm, w_spatial, b_spatial, w_a_out, w_b_out, w_gate, w_o,
           g_post_mix, g_pre_ffn, w_ffn_in, w_ffn_out, g_post_ffn, g_ple, w_ple_gate, w_ple):
    f32 = np.float32
    bf = ml_dtypes.bfloat16
    A = lambda a: np.ascontiguousarray(np.asarray(a, f32))
    x_prompt = A(x_prompt)
    x_sample = A(x_sample)
    p_prompt = A(p_prompt)
    p_sample = A(p_sample)
    if "nc" not in _CACHE:
        _CACHE["nc"], _CACHE["wlayout"] = build()
    nc = _CACHE["nc"]
    wsrc = dict(w_in=A(w_in)[0], w_gate=A(w_gate)[0], w_a_out=A(w_a_out)[0], w_b_out=A(w_b_out)[0], w_o=A(w_o)[0],
                w_ffn_in=A(w_ffn_in)[0], w_ffn_out=A(w_ffn_out)[0], w_ple_gate=A(w_ple_gate)[0], w_ple=A(w_ple)[0])
    parts = []
    for (wn, r0, nkc, c0) in _CACHE["wlayout"]:
        blk = wsrc[wn][r0:r0 + nkc * 128, c0:c0 + WB].reshape(nkc, 128, WB).transpose(1, 0, 2)
        parts.append(np.ascontiguousarray(blk).reshape(-1))
    wstream = np.concatenate(parts)
    ident = np.eye(128, dtype=f32)
    permm = np.zeros((32, 32), f32)
    for i in range(16):
        permm[i + 16, i] = -1.0
        permm[i, i + 16] = 1.0
    tri = (np.arange(128)[:, None] <= np.arange(128)[None, :]).astype(f32)
    shared = dict(
        wstream=wstream,
        w_spatial=A(w_spatial)[0], b_spatial=A(b_spatial).reshape(1, 1024),
        g_pre_mix=A(g_pre_mix), g_post_mix=A(g_post_mix), g_pre_ffn=A(g_pre_ffn), g_post_ffn=A(g_post_ffn),
        g_ple=A(g_ple), g_vnorm=A(g_vnorm),
        c_ident_bf=ident.astype(bf), c_ident_f=ident, c_perm=permm.astype(bf), c_tri_bf=tri.astype(bf), c_tri_f=tri,
    )
    cP, sP = rope_tables(np.arange(1024))
    ck = np.ascontiguousarray(np.asarray(cache_k, f32)).reshape(1280 * 128, 1024)
    cv = np.ascontiguousarray(np.asarray(cache_v, f32)).reshape(1280 * 128, 1024)
    pt = np.asarray(page_table).astype(np.int32)
    pair = np.zeros((128, 64), f32)
    pair[np.arange(128), np.arange(128) // 2] = 1.0 / 256
    iota = np.tile(np.arange(128, dtype=f32)[:, None], (1, 192))
    cm = np.zeros((4, 32), f32)
    for qq in range(4):
        cm[:qq + 1, qq * 8:(qq + 1) * 8] = 1.0
    shared.update(cache_k=ck, cache_v=cv, c_pair=pair, c_iota=iota, c_cm=cm, c_ones=np.ones((128, 128), f32))
    in_maps = []
    for c in range(8):
        b, half = c // 2, c % 2
        own = x_prompt[b, half * 1024:(half + 1) * 1024]
        xo = np.concatenate([own, x_sample[c]], 0)
        xp = x_prompt[b, 0:1024] if half == 1 else np.zeros((1024, D), f32)
        po = np.concatenate([p_prompt[0, b, half * 1024:(half + 1) * 1024], p_sample[0, c]], 0)
        pos_own = np.concatenate([half * 1024 + np.arange(1024), 16384 + np.arange(4)])
        cO, sO = rope_tables(pos_own)
        negm = np.zeros((4, 8, 8), f32)
        for gi in range(4):
            for n in range(8):
                ok = (n < 4 + gi) and (n >= 4 or half == 1)
                negm[gi, :, n] = 0.0 if ok else NEG
        m = dict(shared)
        m.update(x_own=np.ascontiguousarray(xo), x_prev=np.ascontiguousarray(xp), p_own=np.ascontiguousarray(po),
                 ropeC_own=cO, ropeS_own=sO, ropeC_prev=cP, ropeS_prev=sP, c_negm=negm.reshape(1, 256),
                 ptp=np.ascontiguousarray(pt[c].reshape(64, 2)), pt128=np.ascontiguousarray(pt[c].reshape(128, 1).astype(np.uint32)))
        in_maps.append(m)
    res = run_bass_kernel_spmd(nc, in_maps, core_ids=list(range(8)))
    R = res.results
    y_prompt = np.zeros((4, 2048, D), f32)
    y_sample = np.zeros((8, 4, D), f32)
    k_p = np.zeros((1, 4, 2048, NH, HD), f32)
    v_p = np.zeros((1, 4, 2048, NH, HD), f32)
    k_s = np.zeros((1, 8, 4, NH, HD), f32)
    v_s = np.zeros((1, 8, 4, NH, HD), f32)
    gv_s = np.zeros((1, 8, 4, 1024), f32)
    for c in range(8):
        b, half = c // 2, c % 2
        ok = np.asarray(R[c]["o_k"], f32)
        ov = np.asarray(R[c]["o_v"], f32)
        oy = np.asarray(R[c]["o_y"], f32)
        sl = slice(half * 1024, (half + 1) * 1024)
        k_p[0, b, sl] = ok[:1024].reshape(1024, NH, HD)
        v_p[0, b, sl] = ov[:1024].reshape(1024, NH, HD)
        y_prompt[b, sl] = oy[:1024]
        y_sample[c] = oy[1024:]
        k_s[0, c] = ok[1024:].reshape(4, NH, HD)
        v_s[0, c] = ov[1024:].reshape(4, NH, HD)
        gv_s[0, c] = np.asarray(R[c]["o_gv"], f32)
    return (y_prompt, y_sample, k_p, v_p, k_s, v_s, gv_s)
```
